# Optimizing a Trainium2 kernel written in Bass

```python
import math
import jax
import jax.numpy as jnp
from jax import lax
import numpy as np

D_MODEL = 1024
BATCH = 2
SEQ = 8192
DEPTH = 2
DEC_BATCH = 8
DEC_SEQ = 64
PAST_LEN = 4096

CHUNK = 64
EPS = 1e-6
D_PLE = 256
D_FF = 3072
FFN_CONV = 3
D_LRU = 512
LRU_HEADS = 8
LRU_HEAD_DIM = D_LRU // LRU_HEADS
LRU_CONV = 4
LRU_C = 8.0
GLA_HEADS = 4
GLA_DK = 64
GLA_DV = 128
GLA_KEY = GLA_HEADS * GLA_DK
GLA_VAL = GLA_HEADS * GLA_DV
GLA_RANK = 16
GLA_TAU = 16.0
SSD_HEADS = 8
SSD_HEAD_DIM = 64
SSD_INNER = SSD_HEADS * SSD_HEAD_DIM
SSD_GROUPS = 2
SSD_STATE = 128
SSD_CONV = 4
SSD_XBC = SSD_INNER + 2 * SSD_GROUPS * SSD_STATE
D_MIX = D_LRU + GLA_VAL + SSD_INNER
IN_WIDTHS = (D_LRU, D_LRU, GLA_KEY, GLA_KEY, GLA_VAL, GLA_VAL, GLA_RANK, SSD_INNER, SSD_XBC, SSD_HEADS)
D_IN = D_LRU * 2 + GLA_KEY * 2 + GLA_VAL * 2 + GLA_RANK + SSD_INNER + SSD_XBC + SSD_HEADS

kernel_name = "hybrid_streaming_encoder_step"


def rmsnorm(x, g):
    xf = x.astype(jnp.float32)
    y = xf * lax.rsqrt(jnp.mean(xf * xf, axis=-1, keepdims=True) + EPS)
    return (y * g.astype(jnp.float32)).astype(x.dtype)


def causal_dwconv(x, buf, w, b):
    T = x.shape[1]
    width = w.shape[0]
    xp = jnp.concatenate([buf.astype(x.dtype), x], axis=1)
    y = b
    for j in range(width):
        y = y + xp[:, j:j + T] * w[j]
    return y, xp[:, T:].astype(buf.dtype)


def _linear_combine(left, right):
    a1, b1 = left
    a2, b2 = right
    return a1 * a2, a2 * b1 + b2


def rglru_mixer(xb, gb, conv_buf, h0, conv_w, conv_b, w_r, b_r, w_i, b_i, lam):
    B, T, _ = xb.shape
    f32 = jnp.float32
    xc, new_buf = causal_dwconv(xb, conv_buf, conv_w, conv_b)
    xh = xc.reshape(B, T, LRU_HEADS, LRU_HEAD_DIM)
    r = jax.nn.sigmoid((jnp.einsum('bthi,hij->bthj', xh, w_r).reshape(B, T, D_LRU) + b_r).astype(f32))
    i_g = jax.nn.sigmoid((jnp.einsum('bthi,hij->bthj', xh, w_i).reshape(B, T, D_LRU) + b_i).astype(f32))
    log_a = -LRU_C * jax.nn.softplus(-lam.astype(f32)) * r
    a = jnp.exp(log_a)
    u = jnp.sqrt(-jnp.expm1(2.0 * log_a)) * (i_g * xc.astype(f32))
    u = u.at[:, 0].add(a[:, 0] * h0.astype(f32))
    _, h = lax.associative_scan(_linear_combine, (a, u), axis=1)
    y = h * jax.nn.gelu(gb.astype(f32), approximate=True)
    return y.astype(xb.dtype), new_buf, h[:, -1].astype(h0.dtype)


def gla_mixer(q, k, v, g, lr, S0, w_lr, b_lr, norm_g):
    B, T, _ = q.shape
    L = min(CHUNK, T)
    N = T // L
    f32 = jnp.float32
    log_alpha = jax.nn.log_sigmoid((lr @ w_lr + b_lr).astype(f32)) / GLA_TAU
    shp_k = (B, N, L, GLA_HEADS, GLA_DK)
    qc = q.astype(f32).reshape(shp_k) * (GLA_DK ** -0.5)
    kc = k.astype(f32).reshape(shp_k)
    vc = v.astype(f32).reshape(B, N, L, GLA_HEADS, GLA_DV)
    cum = jnp.cumsum(log_alpha.reshape(shp_k), axis=2)
    last = cum[:, :, -1]
    q_dec = qc * jnp.exp(cum)
    k_inv = kc * jnp.exp(-cum)
    k_end = kc * jnp.exp(last[:, :, None] - cum)
    mask = jnp.tril(jnp.ones((L, L), dtype=bool))
    scores = jnp.where(mask, jnp.einsum('bnihd,bnjhd->bnhij', q_dec, k_inv), 0.0)
    o_intra = jnp.einsum('bnhij,bnjhe->bnihe', scores, vc)
    dS = jnp.einsum('bnjhd,bnjhe->bnhde', k_end, vc)

    def step(S, inp):
        dec, ds = inp
        return dec[..., None] * S + ds, S

    S_last, S_prev = lax.scan(step, S0.astype(f32), (jnp.moveaxis(jnp.exp(last), 1, 0), jnp.moveaxis(dS, 1, 0)))
    S_prev = jnp.moveaxis(S_prev, 0, 1)
    o = o_intra + jnp.einsum('bnihd,bnhde->bnihe', q_dec, S_prev)
    o = o * lax.rsqrt(jnp.mean(o * o, axis=-1, keepdims=True) + EPS)
    o = o.reshape(B, T, GLA_VAL) * norm_g.astype(f32)
    o = o * jax.nn.silu(g.astype(f32))
    return o.astype(q.dtype), S_last.astype(S0.dtype)


def ssd_mixer(z, xbc, dt_raw, conv_buf, S0, conv_w, conv_b, dt_bias, a_log, d_skip, norm_g):
    B, T, _ = z.shape
    L = min(CHUNK, T)
    N = T // L
    f32 = jnp.float32
    xbc_c, new_buf = causal_dwconv(xbc, conv_buf, conv_w, conv_b)
    xbc_c = jax.nn.silu(xbc_c.astype(f32))
    xs, Bm, Cm = jnp.split(xbc_c, [SSD_INNER, SSD_INNER + SSD_GROUPS * SSD_STATE], axis=-1)
    rep = SSD_HEADS // SSD_GROUPS
    xh = xs.reshape(B, N, L, SSD_HEADS, SSD_HEAD_DIM)
    Bh = jnp.repeat(Bm.reshape(B, T, SSD_GROUPS, SSD_STATE), rep, axis=2).reshape(B, N, L, SSD_HEADS, SSD_STATE)
    Ch = jnp.repeat(Cm.reshape(B, T, SSD_GROUPS, SSD_STATE), rep, axis=2).reshape(B, N, L, SSD_HEADS, SSD_STATE)
    dt = jax.nn.softplus(dt_raw.astype(f32) + dt_bias.astype(f32)).reshape(B, N, L, SSD_HEADS)
    dA = dt * -jnp.exp(a_log.astype(f32))
    cum = jnp.cumsum(dA, axis=2)
    last = cum[:, :, -1]
    cum_h = jnp.moveaxis(cum, 2, 3)
    seg = cum_h[..., :, None] - cum_h[..., None, :]
    mask = jnp.tril(jnp.ones((L, L), dtype=bool))
    decay = jnp.where(mask, jnp.exp(jnp.where(mask, seg, 0.0)), 0.0)
    scores = jnp.einsum('bnihs,bnjhs->bnhij', Ch, Bh) * decay * jnp.moveaxis(dt, 2, 3)[..., None, :]
    y = jnp.einsum('bnhij,bnjhp->bnihp', scores, xh)
    w_end = jnp.exp(last[:, :, None] - cum) * dt
    dS = jnp.einsum('bnjh,bnjhs,bnjhp->bnhps', w_end, Bh, xh)

    def step(S, inp):
        dec, ds = inp
        return dec[..., None, None] * S + ds, S

    S_last, S_prev = lax.scan(step, S0.astype(f32), (jnp.moveaxis(jnp.exp(last), 1, 0), jnp.moveaxis(dS, 1, 0)))
    S_prev = jnp.moveaxis(S_prev, 0, 1)
    y = y + jnp.einsum('bnihs,bnhps->bnihp', Ch, S_prev) * jnp.exp(cum)[..., None]
    y = y + d_skip.astype(f32)[:, None] * xh
    y = y.reshape(B, T, SSD_INNER) * jax.nn.silu(z.astype(f32))
    y = rmsnorm(y, norm_g)
    return y.astype(z.dtype), new_buf, S_last.astype(S0.dtype)


def conv_ffn(xn, buf, w_gate, w_up, conv_w, conv_b, w_down):
    g, new_buf = causal_dwconv(xn @ w_gate, buf, conv_w, conv_b)
    return (jax.nn.gelu(g, approximate=True) * (xn @ w_up)) @ w_down, new_buf


def run_trunk(x, p, lru_conv, lru_h, gla_S, ssd_conv, ssd_S, ffn_conv, weights):
    (norm_mix, w_in, lru_conv_w, lru_conv_b, lru_w_r, lru_b_r, lru_w_i, lru_b_i, lru_lambda,
     gla_w_lr, gla_b_lr, gla_norm, ssd_conv_w, ssd_conv_b, ssd_dt_bias, ssd_a_log, ssd_d, ssd_norm,
     w_out, norm_ffn, ffn_w_gate, ffn_w_up, ffn_conv_w, ffn_conv_b, ffn_w_down,
     norm_ple, ple_w_gate, ple_w_proj, norm_final) = weights
    offsets = np.cumsum(IN_WIDTHS)[:-1].tolist()
    n_lc, n_lh, n_gs, n_sc, n_ss, n_fc = [], [], [], [], [], []
    for i in range(DEPTH):
        xn = rmsnorm(x, norm_mix[i])
        (a_x, a_g, b_q, b_k, b_v, b_g, b_lr, c_z, c_xbc, c_dt) = jnp.split(xn @ w_in[i], offsets, axis=-1)
        ya, nbuf_a, nh_a = rglru_mixer(a_x, a_g, lru_conv[i], lru_h[i], lru_conv_w[i], lru_conv_b[i],
                                       lru_w_r[i], lru_b_r[i], lru_w_i[i], lru_b_i[i], lru_lambda[i])
        yb, nS_b = gla_mixer(b_q, b_k, b_v, b_g, b_lr, gla_S[i], gla_w_lr[i], gla_b_lr[i], gla_norm[i])
        yc, nbuf_c, nS_c = ssd_mixer(c_z, c_xbc, c_dt, ssd_conv[i], ssd_S[i], ssd_conv_w[i], ssd_conv_b[i],
                                     ssd_dt_bias[i], ssd_a_log[i], ssd_d[i], ssd_norm[i])
        x = x + jnp.concatenate([ya, yb, yc], axis=-1) @ w_out[i]
        f, nbuf_f = conv_ffn(rmsnorm(x, norm_ffn[i]), ffn_conv[i], ffn_w_gate[i], ffn_w_up[i],
                             ffn_conv_w[i], ffn_conv_b[i], ffn_w_down[i])
        x = x + f
        gate = jax.nn.sigmoid(rmsnorm(x, norm_ple[i]) @ ple_w_gate[i])
        x = x + gate * (p[i] @ ple_w_proj[i])
        n_lc.append(nbuf_a)
        n_lh.append(nh_a)
        n_gs.append(nS_b)
        n_sc.append(nbuf_c)
        n_ss.append(nS_c)
        n_fc.append(nbuf_f)
    y = rmsnorm(x, norm_final)
    return (y, jnp.stack(n_lc), jnp.stack(n_lh), jnp.stack(n_gs), jnp.stack(n_sc), jnp.stack(n_ss), jnp.stack(n_fc))


def setup_inputs(seed: int = 0) -> dict:
    key = jax.random.key(seed)
    keys = jax.random.split(key, 48)
    counter = [0]

    def nk():
        counter[0] += 1
        return keys[counter[0] - 1]

    def normal(shape, scale):
        return jax.random.normal(nk(), shape, jnp.float32) * scale

    def gain(shape):
        return 1.0 + normal(shape, 0.02)

    a_pow = jax.random.uniform(nk(), (DEPTH, D_LRU), jnp.float32, 0.9, 0.999)
    sig = a_pow ** (1.0 / LRU_C)
    dt0 = jnp.exp(jax.random.uniform(nk(), (DEPTH, SSD_HEADS), jnp.float32, math.log(1e-3), math.log(1e-1)))
    return {
        "x_prompt": normal((BATCH, SEQ, D_MODEL), 1.0),
        "x_sample": normal((DEC_BATCH, DEC_SEQ, D_MODEL), 1.0),
        "state_lru_conv": normal((DEPTH, DEC_BATCH, LRU_CONV - 1, D_LRU), 1.0),
        "state_lru_h": normal((DEPTH, DEC_BATCH, D_LRU), 0.5),
        "state_gla": normal((DEPTH, DEC_BATCH, GLA_HEADS, GLA_DK, GLA_DV), 0.1),
        "state_ssd_conv": normal((DEPTH, DEC_BATCH, SSD_CONV - 1, SSD_XBC), 1.0),
        "state_ssd": normal((DEPTH, DEC_BATCH, SSD_HEADS, SSD_HEAD_DIM, SSD_STATE), 0.1),
        "state_ffn_conv": normal((DEPTH, DEC_BATCH, FFN_CONV - 1, D_FF), 1.0),
        "p_prompt": normal((DEPTH, BATCH, SEQ, D_PLE), 1.0),
        "p_sample": normal((DEPTH, DEC_BATCH, DEC_SEQ, D_PLE), 1.0),
        "norm_mix": gain((DEPTH, D_MODEL)),
        "w_in": normal((DEPTH, D_MODEL, D_IN), D_MODEL ** -0.5),
        "lru_conv_w": normal((DEPTH, LRU_CONV, D_LRU), LRU_CONV ** -0.5),
        "lru_conv_b": normal((DEPTH, D_LRU), 0.01),
        "lru_w_r": normal((DEPTH, LRU_HEADS, LRU_HEAD_DIM, LRU_HEAD_DIM), LRU_HEAD_DIM ** -0.5),
        "lru_b_r": normal((DEPTH, D_LRU), 0.01),
        "lru_w_i": normal((DEPTH, LRU_HEADS, LRU_HEAD_DIM, LRU_HEAD_DIM), LRU_HEAD_DIM ** -0.5),
        "lru_b_i": normal((DEPTH, D_LRU), 0.01),
        "lru_lambda": jnp.log(sig) - jnp.log1p(-sig),
        "gla_w_lr": normal((DEPTH, GLA_RANK, GLA_KEY), GLA_RANK ** -0.5),
        "gla_b_lr": normal((DEPTH, GLA_KEY), 0.1),
        "gla_norm": gain((DEPTH, GLA_VAL)),
        "ssd_conv_w": normal((DEPTH, SSD_CONV, SSD_XBC), SSD_CONV ** -0.5),
        "ssd_conv_b": normal((DEPTH, SSD_XBC), 0.01),
        "ssd_dt_bias": dt0 + jnp.log(-jnp.expm1(-dt0)),
        "ssd_a_log": jnp.log(jax.random.uniform(nk(), (DEPTH, SSD_HEADS), jnp.float32, 1.0, 16.0)),
        "ssd_d": gain((DEPTH, SSD_HEADS)),
        "ssd_norm": gain((DEPTH, SSD_INNER)),
        "w_out": normal((DEPTH, D_MIX, D_MODEL), D_MIX ** -0.5),
        "norm_ffn": gain((DEPTH, D_MODEL)),
        "ffn_w_gate": normal((DEPTH, D_MODEL, D_FF), D_MODEL ** -0.5),
        "ffn_w_up": normal((DEPTH, D_MODEL, D_FF), D_MODEL ** -0.5),
        "ffn_conv_w": normal((DEPTH, FFN_CONV, D_FF), FFN_CONV ** -0.5),
        "ffn_conv_b": normal((DEPTH, D_FF), 0.01),
        "ffn_w_down": normal((DEPTH, D_FF, D_MODEL), D_FF ** -0.5),
        "norm_ple": gain((DEPTH, D_MODEL)),
        "ple_w_gate": normal((DEPTH, D_MODEL, D_MODEL), D_MODEL ** -0.5),
        "ple_w_proj": normal((DEPTH, D_PLE, D_MODEL), D_PLE ** -0.5),
        "norm_final": gain((D_MODEL,)),
    }


def reference(x_prompt, x_sample, state_lru_conv, state_lru_h, state_gla, state_ssd_conv, state_ssd, state_ffn_conv,
              p_prompt, p_sample, norm_mix, w_in, lru_conv_w, lru_conv_b, lru_w_r, lru_b_r, lru_w_i, lru_b_i,
              lru_lambda, gla_w_lr, gla_b_lr, gla_norm, ssd_conv_w, ssd_conv_b, ssd_dt_bias, ssd_a_log, ssd_d,
              ssd_norm, w_out, norm_ffn, ffn_w_gate, ffn_w_up, ffn_conv_w, ffn_conv_b, ffn_w_down, norm_ple,
              ple_w_gate, ple_w_proj, norm_final):
    weights = (norm_mix, w_in, lru_conv_w, lru_conv_b, lru_w_r, lru_b_r, lru_w_i, lru_b_i, lru_lambda,
               gla_w_lr, gla_b_lr, gla_norm, ssd_conv_w, ssd_conv_b, ssd_dt_bias, ssd_a_log, ssd_d, ssd_norm,
               w_out, norm_ffn, ffn_w_gate, ffn_w_up, ffn_conv_w, ffn_conv_b, ffn_w_down,
               norm_ple, ple_w_gate, ple_w_proj, norm_final)
    bp = x_prompt.shape[0]
    dt = x_prompt.dtype
    (y_prompt, p_lru_conv, p_lru_h, p_gla, p_ssd_conv, p_ssd, p_ffn_conv) = run_trunk(
        x_prompt, p_prompt,
        jnp.zeros((DEPTH, bp, LRU_CONV - 1, D_LRU), dt),
        jnp.zeros((DEPTH, bp, D_LRU), dt),
        jnp.zeros((DEPTH, bp, GLA_HEADS, GLA_DK, GLA_DV), dt),
        jnp.zeros((DEPTH, bp, SSD_CONV - 1, SSD_XBC), dt),
        jnp.zeros((DEPTH, bp, SSD_HEADS, SSD_HEAD_DIM, SSD_STATE), dt),
        jnp.zeros((DEPTH, bp, FFN_CONV - 1, D_FF), dt),
        weights)
    (y_sample, s_lru_conv, s_lru_h, s_gla, s_ssd_conv, s_ssd, s_ffn_conv) = run_trunk(
        x_sample, p_sample, state_lru_conv, state_lru_h, state_gla, state_ssd_conv, state_ssd, state_ffn_conv,
        weights)
    return (y_prompt, y_sample, p_lru_conv, p_lru_h, p_gla, p_ssd_conv, p_ssd, p_ffn_conv,
            s_lru_conv, s_lru_h, s_gla, s_ssd_conv, s_ssd, s_ffn_conv)
```

```python
import contextlib
import numpy as np
import concourse.bass as bass
import concourse.mybir as mybir
from concourse.bass_utils import run_bass_kernel_spmd

F32 = mybir.dt.float32
BF16 = mybir.dt.bfloat16
AF = mybir.ActivationFunctionType
ALU = mybir.AluOpType
AX = mybir.AxisListType

DEPTH = 2
D = 1024
KC = 8
SEQ = 8192
TS = 512
NSUP = SEQ // TS
_STOP = [99]
FWD = 520
EPS = 1e-6
D_FF = 3072
D_IN = 4120


class Buf:
    __slots__ = ("w", "r", "excl")

    def __init__(self, excl=False):
        self.w = None
        self.r = {}
        self.excl = excl


class FW:
    EPOCH = 16000
    DEPOCH = 1000

    def __init__(self, nc, stack):
        self.nc = nc
        self.stack = stack
        self.prog = {k: [] for k in ("pe", "act", "dve", "pool", "sp")}
        self.count = {}
        self.sems = {}
        self.waited = {k: {} for k in self.prog}
        self.ninstr = 0
        self.bg = []
        self.bg_every = 10
        self.bg_ctr = 0
        self.in_bg = False
        self.cap = None

    def sbuf(self, name, shape, dtype):
        return self.stack.enter_context(self.nc.sbuf_tensor("sb_" + name, list(shape), dtype))

    def psum(self, name, shape, dtype):
        return self.stack.enter_context(self.nc.psum_tensor("pp_" + name, list(shape), dtype))

    def _sem(self, prod, epoch):
        key = (prod, epoch)
        if key not in self.sems:
            self.sems[key] = self.stack.enter_context(self.nc.semaphore("s_%s_%d" % (prod, epoch)))
        return self.sems[key]

    NDS = 16

    @staticmethod
    def _is_dma(prod):
        return prod.startswith("dq")

    def _semval(self, prod, seq):
        if self._is_dma(prod):
            return self._sem(prod, 0), 16 * seq
        e = (seq - 1) // self.EPOCH
        v = (seq - 1) % self.EPOCH + 1
        return self._sem(prod, e), v

    def _deps(self, eng, reads, writes, prod):
        deps = {}

        def add(p, s):
            if deps.get(p, 0) < s:
                deps[p] = s
        for b in reads:
            if b.w is not None:
                add(*b.w)
            if b.excl:
                for p, s in b.r.items():
                    if p != prod:
                        add(p, s)
        for b in writes:
            if b.w is not None and not (b.w[0] == prod and not self._is_dma(prod)):
                add(*b.w)
            for p, s in b.r.items():
                if p == prod and not self._is_dma(prod):
                    continue
                add(p, s)
        if eng == "pe":
            deps.pop("pe", None)
        waits = []
        wd = self.waited[eng]
        for p, s in deps.items():
            if wd.get(p, 0) >= s:
                continue
            wd[p] = s
            waits.append(self._semval(p, s))
        return waits

    def begin_capture(self):
        self.cap = []

    def end_capture(self):
        c = self.cap
        self.cap = None
        return c

    def issue(self, items):
        for it in items:
            self.op(*it)

    def interleave(self, A, B):
        na, nb = len(A), len(B)
        ia = ib = 0
        while ia < na or ib < nb:
            if ib >= nb or (ia < na and ia * nb <= ib * na):
                self.op(*A[ia])
                ia += 1
            else:
                self.op(*B[ib])
                ib += 1

    def op(self, eng, fn, reads=(), writes=()):
        if self.cap is not None:
            self.cap.append((eng, fn, list(reads), list(writes)))
            return
        waits = self._deps(eng, reads, writes, eng)
        seq = self.count.get(eng, 0) + 1
        self.count[eng] = seq
        sem, _ = self._semval(eng, seq)

        def thunk(h, waits=waits, fn=fn, sem=sem):
            for (s, v) in waits:
                h.wait_ge(s, v)
            fn(h).then_inc(sem, 1)
        self.prog[eng].append(thunk)
        for b in writes:
            b.w = (eng, seq)
            b.r = {}
        for b in reads:
            if b.r.get(eng, 0) < seq:
                b.r[eng] = seq
        self.ninstr += 1
        if self.bg and not self.in_bg:
            self.bg_ctr += 1
            if self.bg_ctr >= self.bg_every:
                self.bg_ctr = 0
                self.run_bg(1)

    def run_bg(self, n=None):
        self.in_bg = True
        try:
            k = 0
            while self.bg and (n is None or k < n):
                u = self.bg.pop(0)
                u()
                k += 1
        finally:
            self.in_bg = False

    def dma(self, stream, fn, reads=(), writes=(), queue="sp"):
        i = self.count.get("dma", 0)
        self.count["dma"] = i + 1
        prod = "dq%d" % (i % self.NDS)
        seq = i // self.NDS + 1
        waits = self._deps(queue, reads, writes, prod)
        wd = self.waited[queue]
        if seq > 1 and wd.get(prod, 0) < seq - 1:
            wd[prod] = seq - 1
            waits.append(self._semval(prod, seq - 1))
        sem = self._sem(prod, 0)

        def thunk(h, waits=waits, fn=fn, sem=sem):
            for (s, v) in waits:
                h.wait_ge(s, v)
            fn(h).then_inc(sem, 16)
        self.prog[queue].append(thunk)
        for b in writes:
            b.w = (prod, seq)
            b.r = {}
        for b in reads:
            if b.r.get(prod, 0) < seq:
                b.r[prod] = seq
        self.ninstr += 1

    def finish(self):
        waits = []
        n = self.count.get("dma", 0)
        for q in range(min(self.NDS, n)):
            last = (n - 1 - q) // self.NDS + 1
            waits.append(self._semval("dq%d" % q, last))

        def thunk(h, waits=waits):
            for (s, v) in waits:
                h.wait_ge(s, v)
        self.prog["sp"].append(thunk)
        prog = self.prog
        with self.nc.Block() as block:
            @block.tensor
            def _(h):
                for t in prog["pe"]:
                    t(h)

            @block.scalar
            def _(h):
                for t in prog["act"]:
                    t(h)

            @block.vector
            def _(h):
                for t in prog["dve"]:
                    t(h)

            @block.gpsimd
            def _(h):
                for t in prog["pool"]:
                    t(h)

            @block.sync
            def _(h):
                for t in prog["sp"]:
                    t(h)


def build(do_sample=True, n_sup=NSUP):
    nc = bass.Bass("TRN2", target_bir_lowering=False)

    def din(name, shape):
        return nc.dram_tensor(name, list(shape), F32, kind="ExternalInput").ap()

    def dout(name, shape):
        return nc.dram_tensor(name, list(shape), F32, kind="ExternalOutput").ap()

    xT = {"p": din("xT_p", [D, SEQ]), "s": din("xT_s", [D, 64])}
    pT = {"p": din("pT_p", [DEPTH, 256, SEQ]), "s": din("pT_s", [DEPTH, 256, 64])}
    st_in = dict(lc=din("s_lc", [DEPTH, 128, 4, 3]), lh=din("s_lh", [DEPTH, 128, 4]),
                 gla=din("s_gla", [DEPTH, 64, 4, 128]), sc=din("s_sc", [DEPTH, 128, 8, 3]),
                 ssd=din("s_ssd", [DEPTH, 128, 8, 64]), fc=din("s_fc", [DEPTH, 128, 24, 2]))
    w_in = din("w_in", [DEPTH, D, D_IN])
    w_out = din("w_out", [DEPTH, 1536, D])
    w_gate = din("w_gate", [DEPTH, D, D_FF])
    w_up = din("w_up", [DEPTH, D, D_FF])
    w_down = din("w_down", [DEPTH, D_FF, D])
    ple_wg = din("ple_wg", [DEPTH, D, D])
    ple_wp = din("ple_wp", [DEPTH, 256, D])
    d_norms = din("norms", [DEPTH, 128, 3, 8])
    d_nfinal = din("nfinal", [128, 8])
    d_lru = din("lru_small", [DEPTH, 128, 4, 8])
    d_wr = din("wr_bd", [DEPTH, 128, 4, 128])
    d_wi = din("wi_bd", [DEPTH, 128, 4, 128])
    d_wlr = din("wlr", [DEPTH, 16, 256])
    d_blr = din("blr", [DEPTH, 64, 4])
    d_gnorm = din("gnorm", [DEPTH, 128, 4])
    d_scw = din("ssd_cw", [DEPTH, 128, 8, 5])
    d_sbc = din("ssd_bc", [DEPTH, 64, 3, 8])
    d_snorm = din("snorm", [DEPTH, 128, 4])
    d_fcw = din("ffn_cw", [DEPTH, 128, 24, 4])
    d_ident = din("c_ident", [128, 128])
    d_mask = din("c_mask", [64, 64])
    d_U = din("c_U", [64, 64])
    d_diag = din("c_diag", [8, 512])
    d_mneg = din("c_mneg", [64, 512])
    d_sel = din("c_sel", [64, 128])
    d_rmask = din("c_rmask", [64, TS])
    d_segmask = din("segmask", [128, 4])

    yT = {"p": dout("yT_p", [D, SEQ // 4]), "s": dout("yT_s", [D, 64])}
    st_out = {}
    for sk in ("p", "s"):
        st_out[sk] = dict(lc=dout("o_lc_" + sk, [DEPTH, 128, 4, 3]), lh=dout("o_lh_" + sk, [DEPTH, 128, 4]),
                          gla=dout("o_gla_" + sk, [DEPTH, 64, 4, 128]), sc=dout("o_sc_" + sk, [DEPTH, 128, 8, 3]),
                          ssd=dout("o_ssd_" + sk, [DEPTH, 128, 8, 64]), fc=dout("o_fc_" + sk, [DEPTH, 128, 24, 2]))

    with contextlib.ExitStack() as stack:
        fw = FW(nc, stack)
        op, dma = fw.op, fw.dma

        def T_(name, shape, dt=F32):
            return fw.sbuf(name, shape, dt), Buf()

        ident, b_ident = T_("ident", [128, 128])
        identb, b_identb = T_("identb", [128, 128], BF16)
        onesb, b_onesb = T_("onesb", [128, 128], BF16)
        ones8, b_ones8 = T_("ones8", [8, 64])
        maskT, b_mask = T_("maskT", [64, 64])
        Umat, b_U = T_("Umat", [64, 64])
        diag, b_diag = T_("diag", [8, 512])
        mneg, b_mneg = T_("mneg", [64, 512])
        sel63, b_sel = T_("sel63", [64, 128])
        rmask, b_rmask = T_("rmask", [64, TS])
        nfinal, b_nfinal = T_("nfinal", [128, 8])
        segmask, b_segmask = T_("segmask", [128, 4])
        for (t, b, src) in ((ident, b_ident, d_ident), (maskT, b_mask, d_mask), (Umat, b_U, d_U), (diag, b_diag, d_diag),
                            (mneg, b_mneg, d_mneg), (sel63, b_sel, d_sel), (rmask, b_rmask, d_rmask), (nfinal, b_nfinal, d_nfinal), (segmask, b_segmask, d_segmask)):
            dma("c", lambda h, t=t, src=src: h.dma_start(out=t[:], in_=src), writes=[b])
        op("dve", lambda h: h.tensor_copy(out=identb[:], in_=ident[:]), reads=[b_ident], writes=[b_identb])
        mnegb, b_mnegb = T_("mnegb", [64, 512], BF16)
        op("dve", lambda h: h.tensor_copy(out=mnegb[:], in_=mneg[:]), reads=[b_mneg], writes=[b_mnegb])
        op("pool", lambda h: h.memset(onesb[:], 1.0), writes=[b_onesb])
        op("pool", lambda h: h.memset(ones8[:], 1.0), writes=[b_ones8])

        LP = []
        for l in range(DEPTH):
            P = {}
            for nm, shape, src in (("norms", [128, 3, 8], d_norms[l]), ("lru", [128, 4, 8], d_lru[l]),
                                   ("wr", [128, 4, 128], d_wr[l]), ("wi", [128, 4, 128], d_wi[l]),
                                   ("wlr", [16, 256], d_wlr[l]), ("blr", [64, 4], d_blr[l]), ("gnorm", [128, 4], d_gnorm[l]),
                                   ("scw", [128, 8, 5], d_scw[l]), ("sbc", [64, 3, 8], d_sbc[l]), ("snorm", [128, 4], d_snorm[l]),
                                   ("fcw", [128, 24, 4], d_fcw[l])):
                t, b = T_("%s%d" % (nm, l), shape)
                dma("c", lambda h, t=t, src=src: h.dma_start(out=t[:], in_=src), writes=[b])
                P[nm] = t
                P["b_" + nm] = b
            cA, b_cA = T_("cA%d" % l, [128, 4])
            op("act", lambda h, P=P, cA=cA: h.activation(out=cA[:], in_=P["lru"][:, :, 7], func=AF.Softplus, scale=-1.0),
               reads=[P["b_lru"]], writes=[b_cA])
            op("dve", lambda h, cA=cA: h.tensor_scalar(out=cA[:], in0=cA[:], scalar1=-8.0, scalar2=None, op0=ALU.mult),
               reads=[b_cA], writes=[b_cA])
            P["cA"], P["b_cA"] = cA, b_cA
            nblr, b_nblr = T_("nblr%d" % l, [64, 4])
            op("dve", lambda h, P=P, nblr=nblr: h.tensor_scalar(out=nblr[:], in0=P["blr"][:], scalar1=-1.0, scalar2=None, op0=ALU.mult),
               reads=[P["b_blr"]], writes=[b_nblr])
            P["nblr"], P["b_nblr"] = nblr, b_nblr
            aneg, b_aneg = T_("aneg%d" % l, [64, 8])
            op("act", lambda h, P=P, aneg=aneg: h.activation(out=aneg[:], in_=P["sbc"][:, 1, :], func=AF.Exp),
               reads=[P["b_sbc"]], writes=[b_aneg])
            op("dve", lambda h, aneg=aneg: h.tensor_scalar(out=aneg[:], in0=aneg[:], scalar1=-1.0, scalar2=None, op0=ALU.mult),
               reads=[b_aneg], writes=[b_aneg])
            P["aneg"], P["b_aneg"] = aneg, b_aneg
            idd, b_idd = T_("idd%d" % l, [64, 512], BF16)
            op("dve", lambda h, P=P, idd=idd: h.tensor_tensor(out=idd[:].rearrange("p (a b) -> p a b", a=8),
                                                              in0=ident[0:64, 0:64].unsqueeze(1).broadcast_to([64, 8, 64]),
                                                              in1=P["sbc"][:, 2, :].unsqueeze(2).broadcast_to([64, 8, 64]), op=ALU.mult),
               reads=[b_ident, P["b_sbc"]], writes=[b_idd])
            P["idd"], P["b_idd"] = idd, b_idd
            LP.append(P)

        x, _ = T_("x", [128, KC, TS])
        bx = [Buf() for _ in range(KC)]
        xn, b_xn = T_("xn", [128, KC, TS], BF16)
        Fb = [fw.sbuf("F%d" % i, [128, 4, FWD], F32) for i in range(5)]
        bF = [[Buf() for _ in range(4)] for _ in range(5)]
        Hb = [fw.sbuf("H%d" % i, [128, 4, TS], BF16) for i in range(3)]
        bH = [[Buf() for _ in range(4)] for _ in range(3)]
        VT, b_VT = T_("VT", [64, 8, 512], BF16)
        pTb, b_pTb = T_("pTb", [128, 2, TS], BF16)
        NST, NBF = 1, 4
        stg = [fw.sbuf("stg%d" % i, [128, 2048], F32) for i in range(NST)]
        b_stg = [Buf() for _ in range(NST)]
        b_stgh = [Buf(), Buf()]
        wbf = [fw.sbuf("wbf%d" % i, [128, 2048], BF16) for i in range(NBF)]
        b_wbf = [Buf() for _ in range(NBF)]
        wctr = [0]
        PS = [fw.psum("ps%d" % i, [128, 512], F32) for i in range(6)]
        bPS = [Buf(excl=True) for _ in range(6)]
        PSB = [fw.psum("psb%d" % i, [128, 1024], BF16) for i in range(2)]
        bPSB = [Buf(excl=True) for _ in range(2)]
        pctr = [0]

        pools = {"fg": [0, 1, 2, 3], "bg": [0, 1, 2, 3]}
        pctr_bg = [0]

        def next_ps():
            if fw.in_bg:
                pl = pools["bg"]
                i = pl[pctr_bg[0] % len(pl)]
                pctr_bg[0] += 1
            else:
                pl = pools["fg"]
                i = pl[pctr[0] % len(pl)]
                pctr[0] += 1
            return PS[i], bPS[i]

        rs_a, b_rs_a = T_("rs_a", [128, 512])
        rs_b, b_rs_b = T_("rs_b", [128, 512])
        lrT, b_lrT = T_("lrT", [16, TS])
        KT = [T_("KT%d" % i, [64, 256], BF16) for i in range(1)]
        SC = [T_("SC%d" % i, [64, 512], BF16) for i in range(1)]
        ON = [T_("ON%d" % i, [64, 512], BF16) for i in range(1)]
        sm = [T_("sm%d" % i, [64, 16]) for i in range(1)]
        XT = [T_("XT%d" % i, [64, 512], BF16) for i in range(1)]
        XD = [T_("XD%d" % i, [64, 512], BF16) for i in range(1)]
        XW = [T_("XW%d" % i, [64, 512], BF16) for i in range(1)]
        BK = [T_("BK%d" % i, [64, 256], BF16) for i in range(1)]
        EE = [T_("EE%d" % i, [64, 512]) for i in range(1)]
        SQ = EE
        CD = [T_("CD%d" % i, [8, 512]) for i in range(1)]
        Y1 = [T_("Y1%d" % i, [64, 512]) for i in range(1)]
        Y2 = [T_("Y2%d" % i, [64, 512]) for i in range(1)]
        ZS, b_ZS = VT, b_VT
        DT, b_DT = T_("DT", [64, 8, 8])
        DA, b_DA = T_("DA", [64, 8, 8])
        CUM, b_CUM = T_("CUM", [64, 8, 8])
        ECUM, b_ECUM = T_("ECUM", [64, 8, 8])
        WEND, b_WEND = T_("WEND", [64, 8, 8])
        CUMT, b_CUMT = T_("CUMT", [8, TS])
        NCUMT, b_NCUMT = T_("NCUMT", [8, TS])
        DLB, b_DLB = T_("DLB", [128, 8, 8])

        def mk_state(sk):
            S = []
            for l in range(DEPTH):
                d = {}
                for nm, shape, dt in (("lc", [128, 4, 3], F32), ("lh", [128, 4], F32), ("gla", [64, 4, 128], F32),
                                      ("glab", [64, 4, 128], BF16), ("glab2", [64, 4, 128], BF16), ("sc", [128, 8, 3], F32), ("ssd", [128, 8, 64], F32),
                                      ("ssdb", [128, 8, 64], BF16), ("ssdb2", [128, 8, 64], BF16), ("fc", [128, 24, 2], F32)):
                    t, b = T_("st_%s_%s%d" % (sk, nm, l), shape, dt)
                    d[nm], d["b_" + nm] = t, b
                S.append(d)
            return S

        NCACHE = 144
        wcache = nc.dram_tensor("wcache", [NCACHE, 128, 2048], BF16, kind="Internal").ap()
        b_cache = [Buf() for _ in range(NCACHE)]
        wmode = {"idx": 0, "first": True}

        wkeys = {}

        def wslot(src_ap, a, bcols):
            i = wctr[0]
            wctr[0] += 1
            key = (src_ap.tensor.name, src_ap.offset, a, bcols)
            first = key not in wkeys
            if first:
                wkeys[key] = len(wkeys)
            idx = wkeys[key]
            w_t, w_b = wbf[i % NBF], b_wbf[i % NBF]
            n = a * bcols
            if first:
                s_t = stg[0]
                ah = a // 2 if a >= 2 else a
                parts = [(0, ah)] + ([(ah, a)] if ah < a else [])
                for pi_, (a0, a1) in enumerate(parts):
                    n0, n1 = a0 * bcols, a1 * bcols
                    sb_ = b_stgh[pi_]
                    sv = s_t[:, 1024 * pi_:1024 * pi_ + (n1 - n0)].rearrange("p (a b) -> p a b", a=a1 - a0)
                    dma("w", lambda h, sv=sv, a0=a0, a1=a1: h.dma_start(out=sv, in_=src_ap[:, a0:a1, :]), writes=[sb_])
                    op("act", lambda h, pi_=pi_, n0=n0, n1=n1: h.copy(out=w_t[:, n0:n1], in_=s_t[:, 1024 * pi_:1024 * pi_ + (n1 - n0)]),
                       reads=[sb_], writes=[w_b])
                dma("w", lambda h: h.dma_start(out=wcache[idx, :, 0:n], in_=w_t[:, 0:n]), reads=[w_b], writes=[b_cache[idx]])
            else:
                dma("w", lambda h: h.dma_start(out=w_t[:, 0:n], in_=wcache[idx, :, 0:n]), reads=[b_cache[idx]], writes=[w_b])
            return w_t[:, 0:n].rearrange("p (a b) -> p a b", a=a), w_b

        def win_slot(l, m0, mw):
            return wslot(w_in[l, :, m0:m0 + mw].rearrange("(kc p) m -> p kc m", p=128), KC, mw)

        def tiles_of(T):
            return [(c0, min(512, T - c0)) for c0 in range(0, T, 512)]

        def proj(wv, wb, nk, mblocks, rhs, rhs_bufs, T, evac):
            for mi, (mo, mw) in enumerate(mblocks):
                for (c0, n) in tiles_of(T):
                    ps, pb = next_ps()
                    for k in range(nk):
                        op("pe", lambda h, ps=ps, k=k, mo=mo, mw=mw, c0=c0, n=n: h.matmul(
                            ps[0:mw, 0:n], lhsT=wv[:, k, mo:mo + mw], rhs=rhs[:, k, c0:c0 + n],
                            start=(k == 0), stop=(k == nk - 1)), reads=[wb] + rhs_bufs, writes=[pb])
                    evac(mi, c0, n, ps, pb)

        def rmsnorm(gt, gb, gsel, T, dst, dst_bufs):
            for (c0, n) in tiles_of(T):
                for hf in range(2):
                    op("act", lambda h, c0=c0, n=n, hf=hf: h.activation(out=Hb[1 + hf][:, :, 0:n], in_=x[:, 4 * hf:4 * hf + 4, c0:c0 + n], func=AF.Square),
                       reads=bx[4 * hf:4 * hf + 4], writes=bH[1 + hf])
                ps, pb = next_ps()
                for k in range(KC):
                    op("pe", lambda h, ps=ps, k=k, n=n: h.matmul(ps[:, 0:n], lhsT=onesb[:], rhs=Hb[1 + k // 4][:, k % 4, 0:n],
                                                                  start=(k == 0), stop=(k == KC - 1)),
                       reads=[b_onesb, bH[1 + k // 4][k % 4]], writes=[pb])
                op("act", lambda h, ps=ps, n=n: h.activation(out=rs_a[:, 0:n], in_=ps[:, 0:n], func=AF.Sqrt, scale=1.0 / D, bias=EPS),
                   reads=[pb], writes=[b_rs_a])
                op("dve", lambda h, n=n: h.reciprocal(out=rs_b[:, 0:n], in_=rs_a[:, 0:n]), reads=[b_rs_a], writes=[b_rs_b])
                for k in range(KC):
                    op("dve", lambda h, k=k, c0=c0, n=n: h.scalar_tensor_tensor(
                        out=dst[:, k, c0:c0 + n], in0=x[:, k, c0:c0 + n], scalar=gsel(k), in1=rs_b[:, 0:n],
                        op0=ALU.mult, op1=ALU.mult), reads=[bx[k], gb, b_rs_b], writes=dst_bufs)

        def run_pass(sk, T, col0, S, load_state, store_state, so2=False, store_y=True, ycol0=None, load_x=True, prefetch=None):
            nch = T // 64
            wmode["idx"] = 0
            if load_x:
                dma("io", lambda h: h.dma_start(out=x[:, :, 0:T], in_=xT[sk][:, col0:col0 + T].rearrange("(kc p) t -> p kc t", p=128)),
                    writes=bx)
            if load_state:
                for l in range(DEPTH):
                    for nm in ("lc", "lh", "gla", "sc", "ssd", "fc"):
                        dma("io", lambda h, l=l, nm=nm: h.dma_start(out=S[l][nm][:], in_=st_in[nm][l]), writes=[S[l]["b_" + nm]])
                    op("act", lambda h, l=l: h.copy(out=S[l]["glab"][:], in_=S[l]["gla"][:]), reads=[S[l]["b_gla"]], writes=[S[l]["b_glab"]])
                    op("act", lambda h, l=l: h.copy(out=S[l]["ssdb"][:], in_=S[l]["ssd"][:]), reads=[S[l]["b_ssd"]], writes=[S[l]["b_ssdb"]])

            def layer_body(l, P, Sl, so):
                fc_only = (so == "fc")
                so = (so is True)
                def proj_w(src2d, col0, ncols, mw, evac, U=None):
                    mi_base = 0
                    units = []
                    for s0 in range(0, ncols, 256):
                        sw = min(256, ncols - s0)
                        blocks = [(mo, min(mw, sw - mo)) for mo in range(0, sw, mw)]
                        holder = {}

                        def mk(mi, mo, mwid, first, s0=s0, sw=sw, base=mi_base, holder=holder):
                            def u():
                                if first:
                                    holder["w"] = wslot(src2d[:, col0 + s0:col0 + s0 + sw].rearrange("(kc p) m -> p kc m", p=128), KC, sw)
                                wv, wb = holder["w"]

                                def ev(mi_, c0, n, ps, pb):
                                    evac(base + mi, c0, n, ps, pb)
                                proj(wv, wb, KC, [(mo, mwid)], xn, [b_xn], T, ev)
                            return u
                        for mi, (mo, mwid) in enumerate(blocks):
                            units.append(mk(mi, mo, mwid, mi == 0))
                        mi_base += len(blocks)
                    if U is None:
                        for u in units:
                            u()
                    else:
                        U.extend(units)

                def proj_tok(src2d, col0, dst, dbuf, func):
                    ws = [wslot(src2d[512 * kh:512 * kh + 512, col0:col0 + 512].rearrange("(kc p) m -> p kc m", p=128), 4, 512) for kh in range(2)]
                    for j in range(nch):
                        ps, pb = next_ps()
                        for k in range(KC):
                            wv, wb = ws[k // 4]
                            op("pe", lambda h, ps=ps, k=k, j=j, wv=wv: h.matmul(ps[0:64, 0:512], lhsT=xn[:, k, 64 * j:64 * j + 64], rhs=wv[:, k % 4, :],
                                                                                  start=(k == 0), stop=(k == KC - 1)), reads=[wb, b_xn], writes=[pb])
                        op("act", lambda h, ps=ps, j=j: h.activation(out=dst[:, j, :], in_=ps[0:64, 0:512], func=func),
                           reads=[pb], writes=[dbuf])

                def acc_proj(src2d, r0, nk, Y, Yb, U=None):
                    cols = 2048 // nk
                    units = []
                    for cb in range(0, 1024, cols):
                        holder = {}

                        def mk(mm, first, cb=cb, holder=holder):
                            def u():
                                if first:
                                    holder["w"] = wslot(src2d[r0:r0 + nk * 128, cb:cb + cols].rearrange("(kc p) m -> p kc m", p=128), nk, cols)
                                wv, wb = holder["w"]

                                def ev(mi_, c0, n, ps, pb):
                                    m = cb // 128 + mm
                                    op("dve", lambda h: h.tensor_tensor(out=x[:, m, c0:c0 + n], in0=x[:, m, c0:c0 + n], in1=ps[:, 0:n], op=ALU.add),
                                       reads=[pb, bx[m]], writes=[bx[m]])
                                proj(wv, wb, nk, [(mm * 128, 128)], Y, Yb, T, ev)
                            return u
                        for mm in range(cols // 128):
                            units.append(mk(mm, mm == 0))
                    if U is None:
                        for u in units:
                            u()
                    else:
                        U.extend(units)

                if not so:
                    dma("io", lambda h, l=l: h.dma_start(out=Fb[4][:, 0:2, 0:T], in_=pT[sk][l, :, col0:col0 + T].rearrange("(c p) t -> p c t", p=128)),
                        writes=bF[4])
                    op("pool", lambda h: h.tensor_copy(out=pTb[:, :, 0:T], in_=Fb[4][:, 0:2, 0:T]), reads=bF[4], writes=[b_pTb])

                rmsnorm(P["norms"], P["b_norms"], lambda k, P=P: P["norms"][:, 0, k:k + 1], T, xn, [b_xn])

                AXb, XC, R_, IG, A_ = Fb[0], Fb[1], Fb[2], Fb[3], Fb[4]
                YA = Fb[1][:].rearrange("p a b -> p (a b)")[:, 0:1024].bitcast(BF16).rearrange("p (k t) -> p k t", k=4)

                def lru_units(U):
                    def ev_ax(mi, c0, n, ps, pb):
                        op("act", lambda h: h.copy(out=AXb[:, mi, 3 + c0:3 + c0 + n], in_=ps[:, 0:n]), reads=[pb], writes=[bF[0][mi]])
                    proj_w(w_in[l], 0, 512, 128, ev_ax, U)

                    def u_halo():
                        op("dve", lambda h: h.tensor_copy(out=AXb[:, :, 0:3], in_=Sl["lc"][:]), reads=[Sl["b_lc"]], writes=bF[0])
                    U.append(u_halo)

                    def u_conv():
                        for c in range(4):
                            op("act", lambda h, c=c: h.activation(out=XC[:, c, 0:T], in_=AXb[:, c, 0:T], func=AF.Identity,
                                                                  scale=P["lru"][:, c, 0:1], bias=P["lru"][:, c, 4:5]),
                               reads=[bF[0][c], P["b_lru"]], writes=[bF[1][c]])
                        for j in range(1, 4):
                            for c in range(4):
                                op("dve", lambda h, j=j, c=c: h.scalar_tensor_tensor(out=XC[:, c, 0:T], in0=AXb[:, c, j:j + T],
                                                                                      scalar=P["lru"][:, c, j:j + 1], in1=XC[:, c, 0:T],
                                                                                      op0=ALU.mult, op1=ALU.add),
                                   reads=[bF[0][c], bF[1][c], P["b_lru"]], writes=[bF[1][c]])
                    U.append(u_conv)

                    def u_halo_out():
                        op("pool", lambda h: h.tensor_copy(out=Sl["lc"][:], in_=AXb[:, :, T:T + 3]), reads=bF[0], writes=[Sl["b_lc"]])
                    U.append(u_halo_out)

                    def mk_gate(c, which):
                        def u():
                            wt, bw, dst, bd, bcol = (P["wr"], P["b_wr"], R_, bF[2], 5) if which == 0 else (P["wi"], P["b_wi"], IG, bF[3], 6)
                            for (c0, n) in tiles_of(T):
                                ps, pb = next_ps()
                                op("pe", lambda h, ps=ps, c0=c0, n=n: h.matmul(ps[:, 0:n], lhsT=wt[:, c, :], rhs=XC[:, c, c0:c0 + n],
                                                                                 start=True, stop=True), reads=[bw, bF[1][c]], writes=[pb])
                                op("act", lambda h, ps=ps, c0=c0, n=n: h.activation(out=dst[:, c, c0:c0 + n], in_=ps[:, 0:n], func=AF.Sigmoid,
                                                                                      bias=P["lru"][:, c, bcol:bcol + 1]), reads=[pb, P["b_lru"]], writes=[bd[c]])
                        return u
                    for c in range(4):
                        U.append(mk_gate(c, 0))
                        U.append(mk_gate(c, 1))

                    def u_el():
                        C4 = range(4)
                        for c in C4:
                            op("act", lambda h, c=c: h.activation(out=A_[:, c, 0:T], in_=R_[:, c, 0:T], func=AF.Exp, scale=P["cA"][:, c:c + 1]),
                               reads=[bF[2][c], P["b_cA"]], writes=[bF[4][c]])
                        for c in C4:
                            op("dve", lambda h, c=c: h.tensor_tensor(out=AXb[:, c, 0:T], in0=A_[:, c, 0:T], in1=A_[:, c, 0:T], op=ALU.mult),
                               reads=[bF[4][c]], writes=[bF[0][c]])
                        for c in C4:
                            op("act", lambda h, c=c: h.activation(out=AXb[:, c, 0:T], in_=AXb[:, c, 0:T], func=AF.Sqrt, scale=-1.0, bias=1.0),
                               reads=[bF[0][c]], writes=[bF[0][c]])
                        for c in C4:
                            op("dve", lambda h, c=c: h.tensor_tensor(out=IG[:, c, 0:T], in0=IG[:, c, 0:T], in1=XC[:, c, 0:T], op=ALU.mult),
                               reads=[bF[3][c], bF[1][c]], writes=[bF[3][c]])
                        for c in C4:
                            op("dve", lambda h, c=c: h.tensor_tensor(out=IG[:, c, 0:T], in0=IG[:, c, 0:T], in1=AXb[:, c, 0:T], op=ALU.mult),
                               reads=[bF[3][c], bF[0][c]], writes=[bF[3][c]])
                        for c in C4:
                            op("dve", lambda h, c=c: h.tensor_tensor_scan(out=R_[:, c, 0:T], data0=A_[:, c, 0:T], data1=IG[:, c, 0:T],
                                                                          initial=Sl["lh"][:, c:c + 1], op0=ALU.mult, op1=ALU.add),
                               reads=[bF[4][c], bF[3][c], Sl["b_lh"]], writes=[bF[2][c]])
                        for c in C4:
                            op("dve", lambda h, c=c: h.tensor_copy(out=Sl["lh"][:, c:c + 1], in_=R_[:, c, T - 1:T]), reads=[bF[2][c]], writes=[Sl["b_lh"]])
                    U.append(u_el)

                    def ev_ag(mi, c0, n, ps, pb):
                        op("act", lambda h: h.activation(out=AXb[:, mi, c0:c0 + n], in_=ps[:, 0:n], func=AF.Gelu_apprx_tanh),
                           reads=[pb], writes=[bF[0][mi]])
                        op("dve", lambda h: h.tensor_tensor(out=YA[:, mi, c0:c0 + n], in0=R_[:, mi, c0:c0 + n], in1=AXb[:, mi, c0:c0 + n], op=ALU.mult),
                           reads=[bF[2][mi], bF[0][mi]], writes=bF[1])
                    if not so:
                        proj_w(w_in[l], 512, 512, 128, ev_ag, U)
                        acc_proj(w_out[l], 0, 4, YA, bF[1], U)

                def out_proj(l, r0, Y, Yb):
                    acc_proj(w_out[l], r0, 4, Y, Yb)
                _U = []
                lru_units(_U)
                for _u in _U:
                    _u()
                if so and prefetch is not None:
                    psk, pcol = prefetch
                    dma("io", lambda h: h.dma_start(out=x[:, :, 0:T], in_=xT[psk][:, pcol:pcol + T].rearrange("(kc p) t -> p kc t", p=128)),
                        writes=bx)

                if _STOP[0] < 1:
                    return
                Q, K_, L_, CL, EQ = Fb[0], Fb[1], Fb[2], Fb[3], Fb[4]
                SG, QD, KI = Hb[0], Hb[1], Hb[2]
                def ev_qk(mi, c0, n, ps, pb):
                    dst, db = (Q, bF[0]) if mi < 4 else (K_, bF[1])
                    op("act", lambda h: h.copy(out=dst[0:64, mi % 4, c0:c0 + n], in_=ps[0:64, 0:n]), reads=[pb], writes=[db[mi % 4]])
                if so:
                    proj_w(w_in[l], 1280, 256, 64, lambda mi, c0, n, ps, pb: ev_qk(mi + 4, c0, n, ps, pb))
                else:
                    proj_w(w_in[l], 1024, 512, 64, ev_qk)
                proj_tok(w_in[l], 1536, VT, b_VT, AF.Copy)

                def ev_g(mi, c0, n, ps, pb):
                    rs_, b_rs_ = (rs_a, b_rs_a) if mi % 2 == 0 else (rs_b, b_rs_b)
                    op("act", lambda h: h.activation(out=rs_[:, 0:n], in_=ps[:, 0:n], func=AF.Silu), reads=[pb], writes=[b_rs_])
                    op("dve", lambda h: h.tensor_scalar(out=SG[:, mi, c0:c0 + n], in0=rs_[:, 0:n], scalar1=P["gnorm"][:, mi:mi + 1],
                                                        scalar2=None, op0=ALU.mult), reads=[b_rs_, P["b_gnorm"]], writes=[bH[0][mi]])
                if not so:
                    proj_w(w_in[l], 2048, 512, 128, ev_g)

                def ev_lr(mi, c0, n, ps, pb):
                    op("act", lambda h: h.copy(out=lrT[:, c0:c0 + n], in_=ps[0:16, 0:n]), reads=[pb], writes=[b_lrT])
                proj_w(w_in[l], 2560, 16, 16, ev_lr)
                HS = range(4)
                for hh in HS:
                    for (c0, n) in tiles_of(T):
                        ps, pb = next_ps()
                        op("pe", lambda h, ps=ps, hh=hh, c0=c0, n=n: h.matmul(ps[0:64, 0:n], lhsT=P["wlr"][:, hh * 64:hh * 64 + 64], rhs=lrT[:, c0:c0 + n],
                                                                                start=True, stop=True), reads=[P["b_wlr"], b_lrT], writes=[pb])
                        op("act", lambda h, ps=ps, hh=hh, c0=c0, n=n: h.activation(out=L_[0:64, hh, c0:c0 + n], in_=ps[0:64, 0:n], func=AF.Exp,
                                                                                     scale=-1.0, bias=P["nblr"][:, hh:hh + 1]),
                           reads=[pb, P["b_nblr"]], writes=[bF[2][hh]])
                for hh in HS:
                    op("act", lambda h, hh=hh: h.activation(out=L_[0:64, hh, 0:T], in_=L_[0:64, hh, 0:T], func=AF.Ln, bias=1.0),
                       reads=[bF[2][hh]], writes=[bF[2][hh]])
                for hh in HS:
                    op("dve", lambda h, hh=hh: h.tensor_tensor_scan(out=CL[0:64, hh, 0:T], data0=rmask[:, 0:T], data1=L_[0:64, hh, 0:T],
                                                                     initial=0.0, op0=ALU.mult, op1=ALU.add),
                       reads=[bF[2][hh], b_rmask], writes=[bF[3][hh]])
                for hh in HS:
                    op("act", lambda h, hh=hh: h.activation(out=EQ[0:64, hh, 0:T], in_=CL[0:64, hh, 0:T], func=AF.Exp, scale=-1.0 / 16.0),
                       reads=[bF[3][hh]], writes=[bF[4][hh]])
                for hh in HS:
                    op("act", lambda h, hh=hh: h.activation(out=L_[0:64, hh, 0:T], in_=CL[0:64, hh, 0:T], func=AF.Exp, scale=1.0 / 16.0),
                       reads=[bF[3][hh]], writes=[bF[2][hh]])
                for hh in (HS if not so else []):
                    op("dve", lambda h, hh=hh: h.scalar_tensor_tensor(out=QD[0:64, hh, 0:T], in0=Q[0:64, hh, 0:T], scalar=0.125, in1=EQ[0:64, hh, 0:T],
                                                                       op0=ALU.mult, op1=ALU.mult), reads=[bF[0][hh], bF[4][hh]], writes=[bH[1][hh]])
                for hh in HS:
                    op("dve", lambda h, hh=hh: h.tensor_tensor(out=KI[0:64, hh, 0:T], in0=K_[0:64, hh, 0:T], in1=L_[0:64, hh, 0:T], op=ALU.mult),
                       reads=[bF[1][hh], bF[2][hh]], writes=[bH[2][hh]])
                def ssd_preA_units(U):
                    XB0, XB1, PR0, PR1 = Fb[0], Fb[1], Fb[2], Fb[3]

                    def ev_xb(mi, c0, n, ps, pb):
                        XBh, bXB = (XB0, bF[0]) if mi < 4 else (XB1, bF[1])
                        op("act", lambda h: h.copy(out=XBh[:, mi % 4, 3 + c0:3 + c0 + n], in_=ps[:, 0:n]), reads=[pb], writes=[bXB[mi % 4]])
                    proj_w(w_in[l], 3088, 1024, 128, ev_xb, U)
                    for half in range(2):
                        XBh, bXB = (XB0, bF[0]) if half == 0 else (XB1, bF[1])
                        PRh, bPR = (PR0, bF[2]) if half == 0 else (PR1, bF[3])

                        def u_halo(XBh=XBh, bXB=bXB, half=half):
                            op("dve", lambda h: h.tensor_copy(out=XBh[:, :, 0:3], in_=Sl["sc"][:, 4 * half:4 * half + 4, :]),
                               reads=[Sl["b_sc"]], writes=bXB)
                        U.append(u_halo)

                        def u_tap0(XBh=XBh, bXB=bXB, PRh=PRh, bPR=bPR, half=half):
                            for c in range(4):
                                cc = 4 * half + c
                                op("act", lambda h, c=c, cc=cc: h.activation(out=PRh[:, c, 0:T], in_=XBh[:, c, 0:T], func=AF.Identity,
                                                                             scale=P["scw"][:, cc, 0:1], bias=P["scw"][:, cc, 4:5]),
                                   reads=[bXB[c], P["b_scw"]], writes=[bPR[c]])
                        U.append(u_tap0)

                        def mk_tap(j, XBh=XBh, bXB=bXB, PRh=PRh, bPR=bPR, half=half):
                            def u():
                                for c in range(4):
                                    cc = 4 * half + c
                                    op("dve", lambda h, c=c, cc=cc: h.scalar_tensor_tensor(
                                        out=PRh[:, c, 0:T], in0=XBh[:, c, j:j + T], scalar=P["scw"][:, cc, j:j + 1], in1=PRh[:, c, 0:T],
                                        op0=ALU.mult, op1=ALU.add), reads=[bXB[c], bPR[c], P["b_scw"]], writes=[bPR[c]])
                            return u
                        for j in range(1, 4):
                            U.append(mk_tap(j))

                        def u_save(XBh=XBh, bXB=bXB, half=half):
                            op("pool", lambda h: h.tensor_copy(out=Sl["sc"][:, 4 * half:4 * half + 4, :], in_=XBh[:, :, T:T + 3]),
                               reads=bXB, writes=[Sl["b_sc"]])
                        U.append(u_save)

                YG, bYG = Hb[0], bH[0]
                def u_ssd_dt():
                    wv, wb = win_slot(l, 4112, 8)
                    ps, pb = next_ps()
                    for j in range(nch):
                        for k in range(KC):
                            op("pe", lambda h, ps=ps, k=k, j=j, wv=wv: h.matmul(ps[0:64, 8 * j:8 * j + 8], lhsT=xn[:, k, 64 * j:64 * j + 64], rhs=wv[:, k, 0:8],
                                                                                  start=(k == 0), stop=(k == KC - 1)), reads=[wb, b_xn], writes=[pb])
                    nn = nch * 8
                    op("dve", lambda h, ps=ps: h.tensor_tensor(out=DT[:, 0:nch, :], in0=ps[0:64, 0:nn].rearrange("p (a b) -> p a b", b=8),
                                                               in1=P["sbc"][:, 0:1, :].broadcast_to([64, nch, 8]), op=ALU.add),
                       reads=[pb, P["b_sbc"]], writes=[b_DT])
                    op("act", lambda h: h.activation(out=DT[:, 0:nch, :], in_=DT[:, 0:nch, :], func=AF.Softplus), reads=[b_DT], writes=[b_DT])
                    op("dve", lambda h: h.tensor_tensor(out=DA[:, 0:nch, :], in0=DT[:, 0:nch, :], in1=P["aneg"][:].unsqueeze(1).broadcast_to([64, nch, 8]), op=ALU.mult),
                       reads=[b_DT, P["b_aneg"]], writes=[b_DA])
                    ps, pb = next_ps()
                    ps2, pb2 = next_ps()
                    for j in range(nch):
                        op("pe", lambda h, ps=ps, j=j: h.matmul(ps[0:64, 8 * j:8 * j + 8], lhsT=maskT[:], rhs=DA[:, j, :], start=True, stop=True),
                           reads=[b_mask, b_DA], writes=[pb])
                        op("pe", lambda h, ps2=ps2, j=j: h.matmul(ps2[0:64, 8 * j:8 * j + 8], lhsT=Umat[:], rhs=DA[:, j, :], start=True, stop=True),
                           reads=[b_U, b_DA], writes=[pb2])
                    op("act", lambda h, ps=ps: h.copy(out=CUM[:, 0:nch, :], in_=ps[0:64, 0:nn].rearrange("p (a b) -> p a b", b=8)), reads=[pb], writes=[b_CUM])
                    op("act", lambda h, ps=ps: h.activation(out=ECUM[:, 0:nch, :], in_=ps[0:64, 0:nn].rearrange("p (a b) -> p a b", b=8), func=AF.Exp),
                       reads=[pb], writes=[b_ECUM])
                    op("act", lambda h, ps2=ps2: h.activation(out=WEND[:, 0:nch, :], in_=ps2[0:64, 0:nn].rearrange("p (a b) -> p a b", b=8), func=AF.Exp),
                       reads=[pb2], writes=[b_WEND])
                    op("dve", lambda h: h.tensor_tensor(out=WEND[:, 0:nch, :], in0=WEND[:, 0:nch, :], in1=DT[:, 0:nch, :], op=ALU.mult),
                       reads=[b_WEND, b_DT], writes=[b_WEND])
                    for (c0, n) in (tiles_of(T) if not so else []):
                        ps, pb = next_ps()
                        for jj in range(n // 64):
                            j = c0 // 64 + jj
                            op("pe", lambda h, ps=ps, j=j, jj=jj: h.matmul(ps[0:8, 64 * jj:64 * jj + 64], lhsT=DA[:, j, :], rhs=maskT[:], start=True, stop=True),
                               reads=[b_mask, b_DA], writes=[pb])
                        op("act", lambda h, ps=ps, c0=c0, n=n: h.copy(out=CUMT[:, c0:c0 + n], in_=ps[0:8, 0:n]), reads=[pb], writes=[b_CUMT])
                        op("dve", lambda h, ps=ps, c0=c0, n=n: h.tensor_scalar(out=NCUMT[:, c0:c0 + n], in0=ps[0:8, 0:n], scalar1=-1.0, scalar2=None, op0=ALU.mult),
                           reads=[pb], writes=[b_NCUMT])
                    ps, pb = next_ps()
                    op("pe", lambda h, ps=ps: h.matmul(ps[:, 0:nn], lhsT=sel63[:], rhs=ECUM[:, 0:nch, :].rearrange("p a b -> p (a b)"), start=True, stop=True),
                       reads=[b_sel, b_ECUM], writes=[pb])
                    op("act", lambda h, ps=ps: h.copy(out=DLB[:, 0:nch, :], in_=ps[:, 0:nn].rearrange("p (a b) -> p a b", b=8)), reads=[pb], writes=[b_DLB])

                Ubg = []
                ssd_preA_units(Ubg)
                Ubg.append(u_ssd_dt)
                fw.bg = Ubg
                fw.bg_ctr = 0
                pools["fg"], pools["bg"] = [0, 1], [0, 1]
                GB = [(Sl["glab"], Sl["b_glab"]), (Sl["glab2"], Sl["b_glab2"])]

                def gla_p1(j):
                        cs = slice(64 * j, 64 * j + 64)
                        pq = 0
                        kt, b_kt = KT[pq]
                        sc, b_sc = SC[pq]
                        sq, b_sq = SQ[pq]
                        on, b_on = ON[pq]
                        smt, b_sm = sm[pq]
                        for hh in range(4):
                            op("pe", lambda h, hh=hh, cs=cs: h.transpose(out=PSB[0][0:64, hh * 64:hh * 64 + 64], in_=KI[0:64, hh, cs], identity=identb[0:64, 0:64]),
                               reads=[bH[2][hh], b_identb], writes=[bPSB[0]])
                        op("act", lambda h, kt=kt: h.copy(out=kt[:], in_=PSB[0][0:64, 0:256]), reads=[bPSB[0]], writes=[b_kt])
                        if not so:
                            p_s, b_ps = PS[5], bPS[5]
                            for hh in range(4):
                                op("pe", lambda h, hh=hh, cs=cs, p_s=p_s: h.matmul(p_s[0:64, hh * 64:hh * 64 + 64], lhsT=KI[0:64, hh, cs], rhs=QD[0:64, hh, cs],
                                                                                     start=True, stop=True), reads=[bH[2][hh], bH[1][hh]], writes=[b_ps])
                            op("dve", lambda h, p_s=p_s, sc=sc: h.tensor_tensor(out=sc[:, 0:256].rearrange("p (a b) -> p a b", a=4),
                                                                                 in0=p_s[0:64, 0:256].rearrange("p (a b) -> p a b", a=4),
                                                                                 in1=maskT[:].unsqueeze(1).broadcast_to([64, 4, 64]), op=ALU.mult),
                               reads=[b_ps, b_mask], writes=[b_sc])
                            p_o, b_po = PS[2 + j % 2], bPS[2 + j % 2]
                            for hh in range(4):
                                op("pe", lambda h, hh=hh, j=j, p_o=p_o, sc=sc: h.matmul(p_o[0:64, hh * 128:hh * 128 + 128], lhsT=sc[:, hh * 64:hh * 64 + 64],
                                                                                          rhs=VT[:, j, hh * 128:hh * 128 + 128], start=(hh == 0), stop=False, skip_group_check=True),
                                   reads=[b_sc, b_VT], writes=[b_po])
                        p_d, b_pd = PS[4], bPS[4]
                        for hh in range(4):
                            op("pe", lambda h, hh=hh, j=j, p_d=p_d, kt=kt: h.matmul(p_d[0:64, hh * 128:hh * 128 + 128], lhsT=kt[:, hh * 64:hh * 64 + 64],
                                                                                      rhs=VT[:, j, hh * 128:hh * 128 + 128], start=True, stop=True),
                               reads=[b_kt, b_VT], writes=[b_pd])
                        op("dve", lambda h, p_d=p_d: h.tensor_tensor(out=Sl["gla"][:], in0=Sl["gla"][:], in1=p_d[0:64, :].rearrange("p (a b) -> p a b", a=4), op=ALU.add),
                           reads=[b_pd, Sl["b_gla"]], writes=[Sl["b_gla"]])
                        op("dve", lambda h, j=j: h.tensor_tensor(out=Sl["gla"][:], in0=Sl["gla"][:],
                                                                  in1=EQ[0:64, :, 64 * j + 63:64 * j + 64].broadcast_to([64, 4, 128]), op=ALU.mult),
                           reads=bF[4] + [Sl["b_gla"]], writes=[Sl["b_gla"]])
                        op("act", lambda h, j=j: h.copy(out=GB[(j + 1) % 2][0][:], in_=Sl["gla"][:]), reads=[Sl["b_gla"]], writes=[GB[(j + 1) % 2][1]])

                def gla_p2(j):
                        cs = slice(64 * j, 64 * j + 64)
                        pq = 0
                        kt, b_kt = KT[pq]
                        sc, b_sc = SC[pq]
                        sq, b_sq = SQ[pq]
                        on, b_on = ON[pq]
                        smt, b_sm = sm[pq]
                        p_o, b_po = PS[2 + j % 2], bPS[2 + j % 2]
                        for hh in range(4):
                            op("pe", lambda h, hh=hh, cs=cs, p_o=p_o: h.matmul(p_o[0:64, hh * 128:hh * 128 + 128], lhsT=QD[0:64, hh, cs],
                                                                                 rhs=GB[j % 2][0][:, hh, :], start=False, stop=True, skip_group_check=True),
                               reads=[bH[1][hh], GB[j % 2][1]], writes=[b_po])
                        op("act", lambda h, p_o=p_o, sq=sq: h.activation(out=sq[:], in_=p_o[0:64, :], func=AF.Square), reads=[b_po], writes=[b_sq])
                        op("dve", lambda h, sq=sq, smt=smt: h.tensor_reduce(out=smt[:, 0:4], in_=sq[:].rearrange("p (a b) -> p a b", a=4), axis=AX.X, op=ALU.add),
                           reads=[b_sq], writes=[b_sm])
                        op("act", lambda h, smt=smt: h.activation(out=smt[:, 4:8], in_=smt[:, 0:4], func=AF.Ln, scale=1.0 / 128.0, bias=EPS),
                           reads=[b_sm], writes=[b_sm])
                        op("act", lambda h, smt=smt: h.activation(out=smt[:, 8:12], in_=smt[:, 4:8], func=AF.Exp, scale=-0.5), reads=[b_sm], writes=[b_sm])
                        op("dve", lambda h, p_o=p_o, on=on, smt=smt: h.tensor_tensor(out=on[:].rearrange("p (a b) -> p a b", a=4),
                                                                                      in0=p_o[0:64, :].rearrange("p (a b) -> p a b", a=4),
                                                                                      in1=smt[:, 8:12].unsqueeze(2).broadcast_to([64, 4, 128]), op=ALU.mult),
                           reads=[b_po, b_sm], writes=[b_on])
                        for hh in range(4):
                            op("pe", lambda h, hh=hh, on=on: h.transpose(out=PSB[1][:, hh * 64:hh * 64 + 64], in_=on[:, hh * 128:hh * 128 + 128], identity=identb[0:64, 0:64]),
                               reads=[b_on, b_identb], writes=[bPSB[1]])
                        op("dve", lambda h, cs=cs: h.tensor_tensor(out=YG[:, :, cs], in0=PSB[1][:, 0:256].rearrange("p (a b) -> p a b", a=4),
                                                                    in1=SG[:, :, cs], op=ALU.mult), reads=[bPSB[1]] + bH[0], writes=bH[0])

                if so:
                    for j in range(nch):
                        gla_p1(j)
                else:
                    fw.begin_capture()
                    gla_p1(0)
                    fw.issue(fw.end_capture())
                    for j in range(nch):
                        fw.begin_capture()
                        gla_p2(j)
                        B_ = fw.end_capture()
                        A_ops = []
                        if j + 1 < nch:
                            fw.begin_capture()
                            gla_p1(j + 1)
                            A_ops = fw.end_capture()
                        fw.interleave(B_, A_ops)
                fw.run_bg()
                pools["fg"], pools["bg"] = [0, 1, 2, 3], [0, 1, 2, 3]
                if not so:
                    out_proj(l, 512, YG, bYG)

                if _STOP[0] < 2:
                    return
                XB0, XB1, PR0, PR1 = Fb[0], Fb[1], Fb[2], Fb[3]
                XS, BC, YC = Hb[0], Hb[1], Hb[2]
                for half in range(2):
                    PRh, bPR = (PR0, bF[2]) if half == 0 else (PR1, bF[3])
                    for c in range(4):
                        dstH, bdH = (XS, bH[0]) if half == 0 else (BC, bH[1])
                        op("act", lambda h, c=c, PRh=PRh, dstH=dstH: h.activation(out=dstH[:, c, 0:T], in_=PRh[:, c, 0:T], func=AF.Silu),
                           reads=[bPR[c]], writes=[bdH[c]])
                if not so:
                    proj_tok(w_in[l], 2576, ZS, b_ZS, AF.Silu)

                fw.bg = []
                pools["fg"], pools["bg"] = [0], [0]
                SB = [(Sl["ssdb"], Sl["b_ssdb"]), (Sl["ssdb2"], Sl["b_ssdb2"])]

                def ssd_p1(j):
                        cs = slice(64 * j, 64 * j + 64)
                        pq = 0
                        xt, b_xt = XT[pq]
                        xd, b_xd = XD[pq]
                        xw, b_xw = XW[pq]
                        bk, b_bk = BK[pq]
                        ee, b_ee = EE[pq]
                        cd, b_cd = CD[pq]
                        y1, b_y1 = Y1[pq]
                        y2, b_y2 = Y2[pq]
                        sc, b_sc = SC[pq]
                        on, b_on = ON[pq]
                        smt, b_sm = sm[pq]
                        for c in range(4):
                            op("pe", lambda h, c=c, cs=cs: h.transpose(out=PSB[0][0:64, c * 128:c * 128 + 128], in_=XS[:, c, cs], identity=identb[:]),
                               reads=[bH[0][c], b_identb], writes=[bPSB[0]])
                        op("act", lambda h, xt=xt: h.copy(out=xt[:], in_=PSB[0][0:64, 0:512]), reads=[bPSB[0]], writes=[b_xt])
                        for c in range(2):
                            op("pe", lambda h, c=c, cs=cs: h.transpose(out=PSB[0][0:64, 512 + c * 128:512 + c * 128 + 128], in_=BC[:, c, cs], identity=identb[:]),
                               reads=[bH[1][c], b_identb], writes=[bPSB[0]])
                        op("act", lambda h, bk=bk: h.copy(out=bk[:], in_=PSB[0][0:64, 512:768]), reads=[bPSB[0]], writes=[b_bk])
                        if not so:
                            op("pool", lambda h, xt=xt, xd=xd, j=j: h.tensor_tensor(out=xd[:].rearrange("p (a b) -> p a b", a=8), in0=xt[:].rearrange("p (a b) -> p a b", a=8),
                                                                                    in1=DT[:, j, :].unsqueeze(2).broadcast_to([64, 8, 64]), op=ALU.mult),
                               reads=[b_xt, b_DT], writes=[b_xd])
                        op("pool", lambda h, xt=xt, xw=xw, j=j: h.tensor_tensor(out=xw[:].rearrange("p (a b) -> p a b", a=8), in0=xt[:].rearrange("p (a b) -> p a b", a=8),
                                                                                in1=WEND[:, j, :].unsqueeze(2).broadcast_to([64, 8, 64]), op=ALU.mult),
                           reads=[b_xt, b_WEND], writes=[b_xw])
                        if not so:
                            p_g, b_pg = PS[5], bPS[5]
                            for g in range(2):
                                op("pe", lambda h, g=g, cs=cs, p_g=p_g: h.matmul(p_g[0:64, g * 64:g * 64 + 64], lhsT=BC[:, g, cs], rhs=BC[:, 2 + g, cs], start=True, stop=True),
                                   reads=[bH[1][g], bH[1][2 + g]], writes=[b_pg])
                            op("dve", lambda h, cd=cd, cs=cs: h.tensor_tensor(out=cd[:].rearrange("p (a b) -> p a b", a=8), in0=diag[:].rearrange("p (a b) -> p a b", a=8),
                                                                               in1=CUMT[:, cs].unsqueeze(1).broadcast_to([8, 8, 64]), op=ALU.mult),
                               reads=[b_diag, b_CUMT], writes=[b_cd])
                            p_e, b_pe = PS[4], bPS[4]
                            op("pe", lambda h, p_e=p_e, cd=cd: h.matmul(p_e[0:64, :], lhsT=ones8[:], rhs=cd[:], start=True, stop=False), reads=[b_ones8, b_cd], writes=[b_pe])
                            op("pe", lambda h, p_e=p_e, cs=cs: h.matmul(p_e[0:64, :], lhsT=NCUMT[:, cs], rhs=diag[:], start=False, stop=False), reads=[b_NCUMT, b_diag], writes=[b_pe])
                            op("pe", lambda h, p_e=p_e: h.matmul(p_e[0:64, :], lhsT=identb[0:64, 0:64], rhs=mnegb[:], start=False, stop=True), reads=[b_identb, b_mnegb], writes=[b_pe])
                            op("act", lambda h, p_e=p_e, ee=ee: h.activation(out=ee[:], in_=p_e[0:64, :], func=AF.Exp), reads=[b_pe], writes=[b_ee])
                            op("dve", lambda h, ee=ee, sc=sc, p_g=p_g: h.tensor_tensor(out=sc[:].rearrange("p (g a b) -> p g a b", g=2, a=4),
                                                                                        in0=ee[:].rearrange("p (g a b) -> p g a b", g=2, a=4),
                                                                                        in1=p_g[0:64, 0:128].rearrange("p (g b) -> p g b", g=2).unsqueeze(2).broadcast_to([64, 2, 4, 64]),
                                                                                        op=ALU.mult), reads=[b_ee, b_pg], writes=[b_sc])
                            p_y, b_py = PS[2 + j % 2], bPS[2 + j % 2]
                            for hh in range(8):
                                op("pe", lambda h, hh=hh, p_y=p_y, sc=sc, xd=xd: h.matmul(p_y[0:64, hh * 64:hh * 64 + 64], lhsT=sc[:, hh * 64:hh * 64 + 64],
                                                                                            rhs=xd[:, hh * 64:hh * 64 + 64], start=True, stop=False),
                                   reads=[b_sc, b_xd], writes=[b_py])
                                op("pe", lambda h, hh=hh, p_y=p_y, xt=xt: h.matmul(p_y[0:64, hh * 64:hh * 64 + 64], lhsT=P["idd"][:, hh * 64:hh * 64 + 64],
                                                                                     rhs=xt[:, hh * 64:hh * 64 + 64], start=False, stop=True),
                                   reads=[P["b_idd"], b_xt], writes=[b_py])
                        p_d, b_pd = PS[5], bPS[5]
                        for g in range(2):
                            op("pe", lambda h, g=g, p_d=p_d, bk=bk, xw=xw: h.matmul(p_d[:, g * 256:g * 256 + 256], lhsT=bk[:, g * 128:g * 128 + 128],
                                                                                      rhs=xw[:, g * 256:g * 256 + 256], start=True, stop=True),
                               reads=[b_bk, b_xw], writes=[b_pd])
                        op("pool", lambda h, j=j: h.tensor_tensor(out=Sl["ssd"][:], in0=Sl["ssd"][:], in1=DLB[:, j, :].unsqueeze(2).broadcast_to([128, 8, 64]), op=ALU.mult),
                           reads=[Sl["b_ssd"], b_DLB], writes=[Sl["b_ssd"]])
                        op("dve", lambda h, p_d=p_d: h.tensor_tensor(out=Sl["ssd"][:], in0=Sl["ssd"][:], in1=p_d[:, :].rearrange("p (a b) -> p a b", a=8), op=ALU.add),
                           reads=[Sl["b_ssd"], b_pd], writes=[Sl["b_ssd"]])
                        op("act", lambda h, j=j: h.copy(out=SB[(j + 1) % 2][0][:], in_=Sl["ssd"][:]), reads=[Sl["b_ssd"]], writes=[SB[(j + 1) % 2][1]])

                def ssd_p2(j):
                        cs = slice(64 * j, 64 * j + 64)
                        pq = 0
                        xt, b_xt = XT[pq]
                        xd, b_xd = XD[pq]
                        xw, b_xw = XW[pq]
                        bk, b_bk = BK[pq]
                        ee, b_ee = EE[pq]
                        cd, b_cd = CD[pq]
                        y1, b_y1 = Y1[pq]
                        y2, b_y2 = Y2[pq]
                        sc, b_sc = SC[pq]
                        on, b_on = ON[pq]
                        smt, b_sm = sm[pq]
                        p_i, b_pi = PS[1], bPS[1]
                        p_y, b_py = PS[2 + j % 2], bPS[2 + j % 2]
                        for g in range(2):
                            op("pe", lambda h, g=g, cs=cs, p_i=p_i: h.matmul(p_i[0:64, g * 256:g * 256 + 256], lhsT=BC[:, 2 + g, cs],
                                                                               rhs=SB[j % 2][0][:, 4 * g:4 * g + 4, :].rearrange("p a b -> p (a b)"), start=True, stop=True),
                               reads=[bH[1][2 + g], SB[j % 2][1]], writes=[b_pi])
                        op("dve", lambda h, p_i=p_i, y1=y1, j=j: h.tensor_tensor(out=y1[:].rearrange("p (a b) -> p a b", a=8), in0=p_i[0:64, :].rearrange("p (a b) -> p a b", a=8),
                                                                                  in1=ECUM[:, j, :].unsqueeze(2).broadcast_to([64, 8, 64]), op=ALU.mult),
                           reads=[b_pi, b_ECUM], writes=[b_y1])
                        op("dve", lambda h, p_y=p_y, y1=y1: h.tensor_tensor(out=y1[:], in0=y1[:], in1=p_y[0:64, :], op=ALU.add), reads=[b_py, b_y1], writes=[b_y1])
                        op("dve", lambda h, y1=y1, j=j: h.tensor_tensor(out=y1[:], in0=y1[:], in1=ZS[:, j, :], op=ALU.mult), reads=[b_y1, b_ZS], writes=[b_y1])
                        op("act", lambda h, y1=y1, y2=y2, smt=smt: h.activation(out=y2[:], in_=y1[:], func=AF.Square, accum_out=smt[:, 12:13]),
                           reads=[b_y1], writes=[b_y2, b_sm])
                        op("act", lambda h, smt=smt: h.activation(out=smt[:, 13:14], in_=smt[:, 12:13], func=AF.Ln, scale=1.0 / 512.0, bias=EPS), reads=[b_sm], writes=[b_sm])
                        op("act", lambda h, smt=smt: h.activation(out=smt[:, 14:15], in_=smt[:, 13:14], func=AF.Exp, scale=-0.5), reads=[b_sm], writes=[b_sm])
                        op("act", lambda h, y1=y1, on=on, smt=smt: h.activation(out=on[:], in_=y1[:], func=AF.Identity, scale=smt[:, 14:15]),
                           reads=[b_y1, b_sm], writes=[b_on])
                        for c in range(4):
                            op("pe", lambda h, c=c, on=on: h.transpose(out=PSB[1][:, c * 64:c * 64 + 64], in_=on[:, c * 128:c * 128 + 128], identity=identb[0:64, 0:64]),
                               reads=[b_on, b_identb], writes=[bPSB[1]])
                        op("dve", lambda h, cs=cs: h.tensor_tensor(out=YC[:, :, cs], in0=PSB[1][:, 0:256].rearrange("p (a b) -> p a b", a=4),
                                                                    in1=P["snorm"][:].unsqueeze(2).broadcast_to([128, 4, 64]), op=ALU.mult),
                           reads=[bPSB[1], P["b_snorm"]], writes=bH[2])

                if so:
                    for j in range(nch):
                        ssd_p1(j)
                else:
                    fw.begin_capture()
                    ssd_p1(0)
                    fw.issue(fw.end_capture())
                    for j in range(nch):
                        fw.begin_capture()
                        ssd_p2(j)
                        B_ = fw.end_capture()
                        A_ops = []
                        if j + 1 < nch:
                            fw.begin_capture()
                            ssd_p1(j + 1)
                            A_ops = fw.end_capture()
                        fw.issue(B_[:2])
                        fw.interleave(A_ops, B_[2:])
                fw.run_bg()
                pools["fg"], pools["bg"] = [0, 1, 2, 3], [0, 1, 2, 3]
                if not so:
                    out_proj(l, 1024, YC, bH[2])

                if _STOP[0] < 3 or so:
                    return
                rmsnorm(P["norms"], P["b_norms"], lambda k, P=P: P["norms"][:, 1, k:k + 1], T, xn, [b_xn])
                GP, GC = Fb[0], Fb[1]
                HHv = [Fb[2 + q][:].rearrange("p a b -> p (a b)")[:, 0:2048].bitcast(BF16).rearrange("p (k t) -> p k t", k=8) for q in range(3)]
                for fs in range(12):
                    f0 = fs * 256
                    hq, ho = fs // 4, 2 * (fs % 4)

                    def ev_gp(mi, c0, n, ps, pb):
                        op("act", lambda h: h.copy(out=GP[:, mi, 2 + c0:2 + c0 + n], in_=ps[:, 0:n]), reads=[pb], writes=[bF[0][mi]])
                    proj_w(w_gate[l], f0, 256, 128, ev_gp)
                    op("dve", lambda h, fs=fs: h.tensor_copy(out=GP[:, 0:2, 0:2], in_=Sl["fc"][:, 2 * fs:2 * fs + 2, :]), reads=[Sl["b_fc"]], writes=bF[0][0:2])
                    if fc_only:
                        op("pool", lambda h, fs=fs: h.tensor_copy(out=Sl["fc"][:, 2 * fs:2 * fs + 2, :], in_=GP[:, 0:2, T:T + 2]), reads=bF[0][0:2], writes=[Sl["b_fc"]])
                        continue
                    for c in range(2):
                        cc = 2 * fs + c
                        op("act", lambda h, c=c, cc=cc: h.activation(out=GC[:, c, 0:T], in_=GP[:, c, 0:T], func=AF.Identity,
                                                                     scale=P["fcw"][:, cc, 0:1], bias=P["fcw"][:, cc, 3:4]),
                           reads=[bF[0][c], P["b_fcw"]], writes=[bF[1][c]])
                    for j in range(1, 3):
                        for c in range(2):
                            cc = 2 * fs + c
                            op("dve", lambda h, c=c, cc=cc, j=j: h.scalar_tensor_tensor(out=GC[:, c, 0:T], in0=GP[:, c, j:j + T], scalar=P["fcw"][:, cc, j:j + 1],
                                                                                         in1=GC[:, c, 0:T], op0=ALU.mult, op1=ALU.add),
                               reads=[bF[0][c], bF[1][c], P["b_fcw"]], writes=[bF[1][c]])
                    for c in range(2):
                        op("act", lambda h, c=c: h.activation(out=GC[:, c, 0:T], in_=GC[:, c, 0:T], func=AF.Gelu_apprx_tanh), reads=[bF[1][c]], writes=[bF[1][c]])
                    op("pool", lambda h, fs=fs: h.tensor_copy(out=Sl["fc"][:, 2 * fs:2 * fs + 2, :], in_=GP[:, 0:2, T:T + 2]), reads=bF[0][0:2], writes=[Sl["b_fc"]])

                    def ev_up(mi, c0, n, ps, pb, hq=hq, ho=ho):
                        op("dve", lambda h: h.tensor_tensor(out=HHv[hq][:, ho + mi, c0:c0 + n], in0=GC[:, mi, c0:c0 + n], in1=ps[:, 0:n], op=ALU.mult),
                           reads=[pb, bF[1][mi]], writes=bF[2 + hq])
                    proj_w(w_up[l], f0, 256, 128, ev_up)
                if fc_only:
                    return
                hh_bufs = bF[2] + bF[3] + bF[4]
                for m in range(8):
                    ps, pb = next_ps()
                    for half in range(2):
                        wv, wb = wslot(w_down[l][1536 * half:1536 * half + 1536, m * 128:m * 128 + 128].rearrange("(kc p) m -> p kc m", p=128), 12, 128)
                        for kk in range(12):
                            kc = 12 * half + kk
                            op("pe", lambda h, ps=ps, wv=wv, kk=kk, kc=kc: h.matmul(ps[:, 0:T], lhsT=wv[:, kk, :], rhs=HHv[kc // 8][:, kc % 8, 0:T],
                                                                                     start=(kc == 0), stop=(kc == 23)), reads=[wb] + hh_bufs, writes=[pb])
                    op("dve", lambda h, ps=ps, m=m: h.tensor_tensor(out=x[:, m, 0:T], in0=x[:, m, 0:T], in1=ps[:, 0:T], op=ALU.add),
                       reads=[pb, bx[m]], writes=[bx[m]])

                if _STOP[0] < 4:
                    return
                rmsnorm(P["norms"], P["b_norms"], lambda k, P=P: P["norms"][:, 2, k:k + 1], T, xn, [b_xn])
                def ev_pg(mi, c0, n, ps, pb):
                    Gt, bG = (Fb[0], bF[0]) if mi < 4 else (Fb[1], bF[1])
                    op("act", lambda h: h.activation(out=Gt[:, mi % 4, c0:c0 + n], in_=ps[:, 0:n], func=AF.Sigmoid), reads=[pb], writes=[bG[mi % 4]])
                proj_w(ple_wg[l], 0, 1024, 128, ev_pg)
                wv, wb = wslot(ple_wp[l].rearrange("(kc p) m -> p kc m", p=128), 2, 1024)

                def ev_pp(mi, c0, n, ps, pb):
                    Gt, bG = (Fb[0], bF[0]) if mi < 4 else (Fb[1], bF[1])
                    rs_, b_rs_ = (rs_a, b_rs_a) if mi % 2 == 0 else (rs_b, b_rs_b)
                    op("dve", lambda h: h.tensor_tensor(out=rs_[:, 0:n], in0=Gt[:, mi % 4, c0:c0 + n], in1=ps[:, 0:n], op=ALU.mult),
                       reads=[pb, bG[mi % 4]], writes=[b_rs_])
                    op("dve", lambda h: h.tensor_tensor(out=x[:, mi, c0:c0 + n], in0=x[:, mi, c0:c0 + n], in1=rs_[:, 0:n], op=ALU.add),
                       reads=[b_rs_, bx[mi]], writes=[bx[mi]])
                proj(wv, wb, 2, [(m * 128, 128) for m in range(8)], pTb, [b_pTb], T, ev_pp)

            for l in range(DEPTH):
                layer_body(l, LP[l], S[l], so2 if l == DEPTH - 1 else False)

            if store_y:
                YF = [Fb[0], Fb[1]]
                yc0 = col0 if ycol0 is None else ycol0

                class _Dst:
                    def __getitem__(self, idx):
                        p, k, c = idx
                        return YF[k // 4][p, k % 4, c]
                rmsnorm(nfinal, b_nfinal, lambda k: nfinal[:, k:k + 1], T, _Dst(), bF[0] + bF[1])
                for half in range(2):
                    dma("io", lambda h, half=half: h.dma_start(
                        out=yT[sk][512 * half:512 * half + 512, yc0:yc0 + T].rearrange("(kc p) t -> p kc t", p=128),
                        in_=YF[half][:, :, 0:T]), reads=bF[half])
            if store_state:
                for l in range(DEPTH):
                    for nm in ("lc", "lh", "gla", "sc", "ssd", "fc"):
                        dma("io", lambda h, l=l, nm=nm: h.dma_start(out=st_out[sk][nm][l], in_=S[l][nm][:]), reads=[S[l]["b_" + nm]])
            wmode["first"] = False

        Sp = mk_state("st")
        if n_sup > 0:
            for l in range(DEPTH):
                for nm in ("lc", "lh", "gla", "glab", "glab2", "sc", "ssd", "ssdb", "ssdb2", "fc"):
                    op("pool", lambda h, l=l, nm=nm: h.memset(Sp[l][nm][:], 0.0), writes=[Sp[l]["b_" + nm]])
            SEGT = NSUP // 4
            pref_done = [False]
            for s in range(n_sup):
                slot = s // SEGT
                last_slot = (slot == 3) or (n_sup < NSUP)
                so2 = (not last_slot)
                if so2 and s == 3 * SEGT - 1:
                    so2 = "fc"
                can_pref = (so2 is True) and (s + 1 < n_sup)
                run_pass("p", TS, s * TS, Sp, False, s == n_sup - 1, so2=so2, store_y=last_slot,
                         ycol0=(s - 3 * SEGT) * TS if n_sup == NSUP else s * TS,
                         load_x=not pref_done[0], prefetch=("p", (s + 1) * TS) if can_pref else None)
                pref_done[0] = can_pref
                if (s + 1) % SEGT == 0 and s + 1 < n_sup:
                    m = (s + 1) // SEGT - 1
                    for l in range(DEPTH):
                        for nm in ("lc", "lh", "gla", "glab", "glab2", "sc", "ssd", "ssdb", "ssdb2", "fc"):
                            t_ = Sp[l][nm]
                            np_ = 64 if nm in ("gla", "glab", "glab2") else 128
                            op("dve", lambda h, t_=t_, m=m, np_=np_: h.tensor_scalar(out=t_[:], in0=t_[:], scalar1=segmask[0:np_, m:m + 1], scalar2=None, op0=ALU.mult),
                               reads=[Sp[l]["b_" + nm], b_segmask], writes=[Sp[l]["b_" + nm]])
        if do_sample:
            run_pass("s", 64, 0, Sp, True, True)
        fw.finish()
        build.ninstr = fw.ninstr
    return nc


def _fm(v, nchunk):
    return np.ascontiguousarray(np.asarray(v, np.float32).reshape(nchunk, 128).T)


def _consts():
    c = {}
    c["c_ident"] = np.eye(128, dtype=np.float32)
    j = np.arange(64)
    c["c_mask"] = (j[:, None] <= j[None, :]).astype(np.float32)
    c["c_U"] = (j[:, None] > j[None, :]).astype(np.float32)
    dg = np.zeros((8, 8, 64), np.float32)
    for h in range(8):
        dg[h, h, :] = 1.0
    c["c_diag"] = dg.reshape(8, 512)
    mn = np.where(j[:, None] <= j[None, :], 0.0, -30000.0).astype(np.float32)
    c["c_mneg"] = np.ascontiguousarray(np.broadcast_to(mn[:, None, :], (64, 8, 64))).reshape(64, 512)
    sel = np.zeros((64, 128), np.float32)
    sel[63, :] = 1.0
    c["c_sel"] = sel
    rm = np.ones((64, TS), np.float32)
    rm[:, ::64] = 0.0
    c["c_rmask"] = rm
    return c


def _prep_shared(inp):
    f = lambda a: np.ascontiguousarray(np.asarray(a, dtype=np.float32))
    sh = {}
    sh["w_in"] = f(inp["w_in"])
    sh["w_out"] = f(inp["w_out"])
    sh["w_gate"] = f(inp["ffn_w_gate"])
    sh["w_up"] = f(inp["ffn_w_up"])
    sh["w_down"] = f(inp["ffn_w_down"])
    sh["ple_wg"] = f(inp["ple_w_gate"])
    sh["ple_wp"] = f(inp["ple_w_proj"])
    norms = np.zeros((DEPTH, 128, 3, 8), np.float32)
    lru = np.zeros((DEPTH, 128, 4, 8), np.float32)
    wr = np.zeros((DEPTH, 128, 4, 128), np.float32)
    wi = np.zeros((DEPTH, 128, 4, 128), np.float32)
    blr = np.zeros((DEPTH, 64, 4), np.float32)
    gnorm = np.zeros((DEPTH, 128, 4), np.float32)
    scw = np.zeros((DEPTH, 128, 8, 5), np.float32)
    sbc = np.zeros((DEPTH, 64, 3, 8), np.float32)
    snorm = np.zeros((DEPTH, 128, 4), np.float32)
    fcw = np.zeros((DEPTH, 128, 24, 4), np.float32)
    for l in range(DEPTH):
        norms[l, :, 0, :] = _fm(inp["norm_mix"][l], 8)
        norms[l, :, 1, :] = _fm(inp["norm_ffn"][l], 8)
        norms[l, :, 2, :] = _fm(inp["norm_ple"][l], 8)
        for j in range(4):
            lru[l, :, :, j] = _fm(inp["lru_conv_w"][l, j], 4)
        lru[l, :, :, 4] = _fm(inp["lru_conv_b"][l], 4)
        lru[l, :, :, 5] = _fm(inp["lru_b_r"][l], 4)
        lru[l, :, :, 6] = _fm(inp["lru_b_i"][l], 4)
        lru[l, :, :, 7] = _fm(inp["lru_lambda"][l], 4)
        for h in range(8):
            c, o = h // 2, (h % 2) * 64
            wr[l, o:o + 64, c, o:o + 64] = inp["lru_w_r"][l, h]
            wi[l, o:o + 64, c, o:o + 64] = inp["lru_w_i"][l, h]
        blr[l] = np.asarray(inp["gla_b_lr"][l], np.float32).reshape(4, 64).T
        gnorm[l] = _fm(inp["gla_norm"][l], 4)
        for j in range(4):
            scw[l, :, :, j] = _fm(inp["ssd_conv_w"][l, j], 8)
        scw[l, :, :, 4] = _fm(inp["ssd_conv_b"][l], 8)
        sbc[l, :, 0, :] = np.asarray(inp["ssd_dt_bias"][l], np.float32)[None, :]
        sbc[l, :, 1, :] = np.asarray(inp["ssd_a_log"][l], np.float32)[None, :]
        sbc[l, :, 2, :] = np.asarray(inp["ssd_d"][l], np.float32)[None, :]
        snorm[l] = _fm(inp["ssd_norm"][l], 4)
        for j in range(3):
            fcw[l, :, :, j] = _fm(inp["ffn_conv_w"][l, j], 24)
        fcw[l, :, :, 3] = _fm(inp["ffn_conv_b"][l], 24)
    sh.update(norms=norms, nfinal=_fm(inp["norm_final"], 8), lru_small=lru, wr_bd=wr, wi_bd=wi, wlr=f(inp["gla_w_lr"]),
              blr=blr, gnorm=gnorm, ssd_cw=scw, ssd_bc=sbc, snorm=snorm, ffn_cw=fcw)
    sh.update(_consts())
    return sh


def _prep_core(inp, c):
    f = lambda a: np.ascontiguousarray(np.asarray(a, dtype=np.float32))
    m = {}
    b, k = c // 4, c % 4
    SEG = SEQ // 4
    segs = [max(s - (3 - k), 0) for s in range(4)]
    xp = np.asarray(inp["x_prompt"][b])
    pp = np.asarray(inp["p_prompt"][:, b])
    m["xT_p"] = f(np.concatenate([xp[g * SEG:(g + 1) * SEG] for g in segs], axis=0).T)
    m["pT_p"] = f(np.transpose(np.concatenate([pp[:, g * SEG:(g + 1) * SEG] for g in segs], axis=1), (0, 2, 1)))
    sm = np.ones((128, 4), np.float32)
    for mb in range(3):
        if mb < 3 - k:
            sm[:, mb] = 0.0
    m["segmask"] = sm
    m["xT_s"] = f(np.asarray(inp["x_sample"][c]).T)
    m["pT_s"] = f(np.transpose(np.asarray(inp["p_sample"][:, c]), (0, 2, 1)))
    lc = np.asarray(inp["state_lru_conv"][:, c], np.float32)
    m["s_lc"] = f(lc.reshape(DEPTH, 3, 4, 128).transpose(0, 3, 2, 1))
    lh = np.asarray(inp["state_lru_h"][:, c], np.float32)
    m["s_lh"] = f(lh.reshape(DEPTH, 4, 128).transpose(0, 2, 1))
    g = np.asarray(inp["state_gla"][:, c], np.float32)
    m["s_gla"] = f(g.transpose(0, 2, 1, 3))
    sc = np.asarray(inp["state_ssd_conv"][:, c], np.float32)
    m["s_sc"] = f(sc.reshape(DEPTH, 3, 8, 128).transpose(0, 3, 2, 1))
    ss = np.asarray(inp["state_ssd"][:, c], np.float32)
    m["s_ssd"] = f(ss.transpose(0, 3, 1, 2))
    fc = np.asarray(inp["state_ffn_conv"][:, c], np.float32)
    m["s_fc"] = f(fc.reshape(DEPTH, 2, 24, 128).transpose(0, 3, 2, 1))
    return m


def _unpack_states(r, sk):
    lc = r["o_lc_" + sk].transpose(0, 3, 2, 1).reshape(DEPTH, 3, 512)
    lh = r["o_lh_" + sk].transpose(0, 2, 1).reshape(DEPTH, 512)
    g = r["o_gla_" + sk].transpose(0, 2, 1, 3)
    sc = r["o_sc_" + sk].transpose(0, 3, 2, 1).reshape(DEPTH, 3, 1024)
    ss = r["o_ssd_" + sk].transpose(0, 2, 3, 1)
    fc = r["o_fc_" + sk].transpose(0, 3, 2, 1).reshape(DEPTH, 2, 3072)
    return [lc, lh, g, sc, ss, fc]


_NC_CACHE = {}


def kernel(**inputs):
    if "nc" not in _NC_CACHE:
        _NC_CACHE["nc"] = build()
    nc = _NC_CACHE["nc"]
    sh = _prep_shared(inputs)
    in_maps = []
    for c in range(8):
        m = dict(sh)
        m.update(_prep_core(inputs, c))
        in_maps.append(m)
    res = run_bass_kernel_spmd(nc, in_maps, core_ids=list(range(8)))
    R = res.results
    y_prompt = np.stack([np.concatenate([R[4 * b + k]["yT_p"].T for k in range(4)], axis=0) for b in range(2)]).astype(np.float32)
    y_sample = np.stack([np.ascontiguousarray(R[c]["yT_s"].T) for c in range(8)]).astype(np.float32)
    pst = [_unpack_states(R[4 * b + 3], "p") for b in range(2)]
    sst = [_unpack_states(R[c], "s") for c in range(8)]
    outs = [y_prompt, y_sample]
    for i in range(6):
        outs.append(np.ascontiguousarray(np.stack([pst[b][i] for b in range(2)], axis=1)).astype(np.float32))
    for i in range(6):
        outs.append(np.ascontiguousarray(np.stack([sst[c][i] for c in range(8)], axis=1)).astype(np.float32))
    return tuple(outs)
```

```python
import contextlib
import numpy as np
import concourse.bass as bass
import concourse.mybir as mybir
from concourse.bass_utils import run_bass_kernel_spmd

F32 = mybir.dt.float32
BF16 = mybir.dt.bfloat16
AF = mybir.ActivationFunctionType
ALU = mybir.AluOpType
AX = mybir.AxisListType

DEPTH = 2
D = 1024
KC = 8
SEQ = 8192
TS = 512
NSUP = SEQ // TS
_STOP = [99]
FWD = 520
EPS = 1e-6
D_FF = 3072
D_IN = 4120


class Buf:
    __slots__ = ("w", "r", "excl")

    def __init__(self, excl=False):
        self.w = None
        self.r = {}
        self.excl = excl


class FW:
    EPOCH = 16000
    DEPOCH = 1000

    def __init__(self, nc, stack):
        self.nc = nc
        self.stack = stack
        self.prog = {k: [] for k in ("pe", "act", "dve", "pool", "sp")}
        self.count = {}
        self.sems = {}
        self.waited = {k: {} for k in self.prog}
        self.ninstr = 0
        self.bg = []
        self.bg_every = 10
        self.bg_ctr = 0
        self.in_bg = False
        self.cap = None

    def sbuf(self, name, shape, dtype):
        return self.stack.enter_context(self.nc.sbuf_tensor("sb_" + name, list(shape), dtype))

    def psum(self, name, shape, dtype):
        return self.stack.enter_context(self.nc.psum_tensor("pp_" + name, list(shape), dtype))

    def _sem(self, prod, epoch):
        key = (prod, epoch)
        if key not in self.sems:
            self.sems[key] = self.stack.enter_context(self.nc.semaphore("s_%s_%d" % (prod, epoch)))
        return self.sems[key]

    NDS = 16

    @staticmethod
    def _is_dma(prod):
        return prod.startswith("dq")

    def _semval(self, prod, seq):
        if self._is_dma(prod):
            return self._sem(prod, 0), 16 * seq
        e = (seq - 1) // self.EPOCH
        v = (seq - 1) % self.EPOCH + 1
        return self._sem(prod, e), v

    def _deps(self, eng, reads, writes, prod):
        deps = {}

        def add(p, s):
            if deps.get(p, 0) < s:
                deps[p] = s
        for b in reads:
            if b.w is not None:
                add(*b.w)
            if b.excl:
                for p, s in b.r.items():
                    if p != prod:
                        add(p, s)
        for b in writes:
            if b.w is not None and not (b.w[0] == prod and not self._is_dma(prod)):
                add(*b.w)
            for p, s in b.r.items():
                if p == prod and not self._is_dma(prod):
                    continue
                add(p, s)
        if eng == "pe":
            deps.pop("pe", None)
        waits = []
        wd = self.waited[eng]
        for p, s in deps.items():
            if wd.get(p, 0) >= s:
                continue
            wd[p] = s
            waits.append(self._semval(p, s))
        return waits

    def begin_capture(self):
        self.cap = []

    def end_capture(self):
        c = self.cap
        self.cap = None
        return c

    def issue(self, items):
        for it in items:
            self.op(*it)

    def interleave(self, A, B):
        na, nb = len(A), len(B)
        ia = ib = 0
        while ia < na or ib < nb:
            if ib >= nb or (ia < na and ia * nb <= ib * na):
                self.op(*A[ia])
                ia += 1
            else:
                self.op(*B[ib])
                ib += 1

    def op(self, eng, fn, reads=(), writes=()):
        if self.cap is not None:
            self.cap.append((eng, fn, list(reads), list(writes)))
            return
        waits = self._deps(eng, reads, writes, eng)
        seq = self.count.get(eng, 0) + 1
        self.count[eng] = seq
        sem, _ = self._semval(eng, seq)

        def thunk(h, waits=waits, fn=fn, sem=sem):
            for (s, v) in waits:
                h.wait_ge(s, v)
            fn(h).then_inc(sem, 1)
        self.prog[eng].append(thunk)
        for b in writes:
            b.w = (eng, seq)
            b.r = {}
        for b in reads:
            if b.r.get(eng, 0) < seq:
                b.r[eng] = seq
        self.ninstr += 1
        if self.bg and not self.in_bg:
            self.bg_ctr += 1
            if self.bg_ctr >= self.bg_every:
                self.bg_ctr = 0
                self.run_bg(1)

    def run_bg(self, n=None):
        self.in_bg = True
        try:
            k = 0
            while self.bg and (n is None or k < n):
                u = self.bg.pop(0)
                u()
                k += 1
        finally:
            self.in_bg = False

    def dma(self, stream, fn, reads=(), writes=(), queue="sp"):
        i = self.count.get("dma", 0)
        self.count["dma"] = i + 1
        prod = "dq%d" % (i % self.NDS)
        seq = i // self.NDS + 1
        waits = self._deps(queue, reads, writes, prod)
        wd = self.waited[queue]
        if seq > 1 and wd.get(prod, 0) < seq - 1:
            wd[prod] = seq - 1
            waits.append(self._semval(prod, seq - 1))
        sem = self._sem(prod, 0)

        def thunk(h, waits=waits, fn=fn, sem=sem):
            for (s, v) in waits:
                h.wait_ge(s, v)
            fn(h).then_inc(sem, 16)
        self.prog[queue].append(thunk)
        for b in writes:
            b.w = (prod, seq)
            b.r = {}
        for b in reads:
            if b.r.get(prod, 0) < seq:
                b.r[prod] = seq
        self.ninstr += 1

    def finish(self):
        waits = []
        n = self.count.get("dma", 0)
        for q in range(min(self.NDS, n)):
            last = (n - 1 - q) // self.NDS + 1
            waits.append(self._semval("dq%d" % q, last))

        def thunk(h, waits=waits):
            for (s, v) in waits:
                h.wait_ge(s, v)
        self.prog["sp"].append(thunk)
        prog = self.prog
        with self.nc.Block() as block:
            @block.tensor
            def _(h):
                for t in prog["pe"]:
                    t(h)

            @block.scalar
            def _(h):
                for t in prog["act"]:
                    t(h)

            @block.vector
            def _(h):
                for t in prog["dve"]:
                    t(h)

            @block.gpsimd
            def _(h):
                for t in prog["pool"]:
                    t(h)

            @block.sync
            def _(h):
                for t in prog["sp"]:
                    t(h)


def build(do_sample=True, n_sup=NSUP):
    nc = bass.Bass("TRN2", target_bir_lowering=False)

    def din(name, shape):
        return nc.dram_tensor(name, list(shape), F32, kind="ExternalInput").ap()

    def dout(name, shape):
        return nc.dram_tensor(name, list(shape), F32, kind="ExternalOutput").ap()

    xT = {"p": din("xT_p", [D, SEQ]), "s": din("xT_s", [D, 64])}
    pT = {"p": din("pT_p", [DEPTH, 256, SEQ]), "s": din("pT_s", [DEPTH, 256, 64])}
    st_in = dict(lc=din("s_lc", [DEPTH, 128, 4, 3]), lh=din("s_lh", [DEPTH, 128, 4]),
                 gla=din("s_gla", [DEPTH, 64, 4, 128]), sc=din("s_sc", [DEPTH, 128, 8, 3]),
                 ssd=din("s_ssd", [DEPTH, 128, 8, 64]), fc=din("s_fc", [DEPTH, 128, 24, 2]))
    w_in = din("w_in", [DEPTH, D, D_IN])
    w_out = din("w_out", [DEPTH, 1536, D])
    w_gate = din("w_gate", [DEPTH, D, D_FF])
    w_up = din("w_up", [DEPTH, D, D_FF])
    w_down = din("w_down", [DEPTH, D_FF, D])
    ple_wg = din("ple_wg", [DEPTH, D, D])
    ple_wp = din("ple_wp", [DEPTH, 256, D])
    d_norms = din("norms", [DEPTH, 128, 3, 8])
    d_nfinal = din("nfinal", [128, 8])
    d_lru = din("lru_small", [DEPTH, 128, 4, 8])
    d_wr = din("wr_bd", [DEPTH, 128, 4, 128])
    d_wi = din("wi_bd", [DEPTH, 128, 4, 128])
    d_wlr = din("wlr", [DEPTH, 16, 256])
    d_blr = din("blr", [DEPTH, 64, 4])
    d_gnorm = din("gnorm", [DEPTH, 128, 4])
    d_scw = din("ssd_cw", [DEPTH, 128, 8, 5])
    d_sbc = din("ssd_bc", [DEPTH, 64, 3, 8])
    d_snorm = din("snorm", [DEPTH, 128, 4])
    d_fcw = din("ffn_cw", [DEPTH, 128, 24, 4])
    d_ident = din("c_ident", [128, 128])
    d_mask = din("c_mask", [64, 64])
    d_U = din("c_U", [64, 64])
    d_diag = din("c_diag", [8, 512])
    d_mneg = din("c_mneg", [64, 512])
    d_sel = din("c_sel", [64, 128])
    d_rmask = din("c_rmask", [64, TS])
    d_segmask = din("segmask", [128, 4])

    yT = {"p": dout("yT_p", [D, SEQ // 4]), "s": dout("yT_s", [D, 64])}
    st_out = {}
    for sk in ("p", "s"):
        st_out[sk] = dict(lc=dout("o_lc_" + sk, [DEPTH, 128, 4, 3]), lh=dout("o_lh_" + sk, [DEPTH, 128, 4]),
                          gla=dout("o_gla_" + sk, [DEPTH, 64, 4, 128]), sc=dout("o_sc_" + sk, [DEPTH, 128, 8, 3]),
                          ssd=dout("o_ssd_" + sk, [DEPTH, 128, 8, 64]), fc=dout("o_fc_" + sk, [DEPTH, 128, 24, 2]))

    with contextlib.ExitStack() as stack:
        fw = FW(nc, stack)
        op, dma = fw.op, fw.dma

        def T_(name, shape, dt=F32):
            return fw.sbuf(name, shape, dt), Buf()

        ident, b_ident = T_("ident", [128, 128])
        identb, b_identb = T_("identb", [128, 128], BF16)
        onesb, b_onesb = T_("onesb", [128, 128], BF16)
        ones8, b_ones8 = T_("ones8", [8, 64])
        maskT, b_mask = T_("maskT", [64, 64])
        Umat, b_U = T_("Umat", [64, 64])
        diag, b_diag = T_("diag", [8, 512])
        mneg, b_mneg = T_("mneg", [64, 512])
        sel63, b_sel = T_("sel63", [64, 128])
        rmask, b_rmask = T_("rmask", [64, TS])
        nfinal, b_nfinal = T_("nfinal", [128, 8])
        segmask, b_segmask = T_("segmask", [128, 4])
        for (t, b, src) in ((ident, b_ident, d_ident), (maskT, b_mask, d_mask), (Umat, b_U, d_U), (diag, b_diag, d_diag),
                            (mneg, b_mneg, d_mneg), (sel63, b_sel, d_sel), (rmask, b_rmask, d_rmask), (nfinal, b_nfinal, d_nfinal), (segmask, b_segmask, d_segmask)):
            dma("c", lambda h, t=t, src=src: h.dma_start(out=t[:], in_=src), writes=[b])
        op("dve", lambda h: h.tensor_copy(out=identb[:], in_=ident[:]), reads=[b_ident], writes=[b_identb])
        mnegb, b_mnegb = T_("mnegb", [64, 512], BF16)
        op("dve", lambda h: h.tensor_copy(out=mnegb[:], in_=mneg[:]), reads=[b_mneg], writes=[b_mnegb])
        op("pool", lambda h: h.memset(onesb[:], 1.0), writes=[b_onesb])
        op("pool", lambda h: h.memset(ones8[:], 1.0), writes=[b_ones8])

        LP = []
        for l in range(DEPTH):
            P = {}
            for nm, shape, src in (("norms", [128, 3, 8], d_norms[l]), ("lru", [128, 4, 8], d_lru[l]),
                                   ("wr", [128, 4, 128], d_wr[l]), ("wi", [128, 4, 128], d_wi[l]),
                                   ("wlr", [16, 256], d_wlr[l]), ("blr", [64, 4], d_blr[l]), ("gnorm", [128, 4], d_gnorm[l]),
                                   ("scw", [128, 8, 5], d_scw[l]), ("sbc", [64, 3, 8], d_sbc[l]), ("snorm", [128, 4], d_snorm[l]),
                                   ("fcw", [128, 24, 4], d_fcw[l])):
                t, b = T_("%s%d" % (nm, l), shape)
                dma("c", lambda h, t=t, src=src: h.dma_start(out=t[:], in_=src), writes=[b])
                P[nm] = t
                P["b_" + nm] = b
            cA, b_cA = T_("cA%d" % l, [128, 4])
            op("act", lambda h, P=P, cA=cA: h.activation(out=cA[:], in_=P["lru"][:, :, 7], func=AF.Softplus, scale=-1.0),
               reads=[P["b_lru"]], writes=[b_cA])
            op("dve", lambda h, cA=cA: h.tensor_scalar(out=cA[:], in0=cA[:], scalar1=-8.0, scalar2=None, op0=ALU.mult),
               reads=[b_cA], writes=[b_cA])
            P["cA"], P["b_cA"] = cA, b_cA
            nblr, b_nblr = T_("nblr%d" % l, [64, 4])
            op("dve", lambda h, P=P, nblr=nblr: h.tensor_scalar(out=nblr[:], in0=P["blr"][:], scalar1=-1.0, scalar2=None, op0=ALU.mult),
               reads=[P["b_blr"]], writes=[b_nblr])
            P["nblr"], P["b_nblr"] = nblr, b_nblr
            aneg, b_aneg = T_("aneg%d" % l, [64, 8])
            op("act", lambda h, P=P, aneg=aneg: h.activation(out=aneg[:], in_=P["sbc"][:, 1, :], func=AF.Exp),
               reads=[P["b_sbc"]], writes=[b_aneg])
            op("dve", lambda h, aneg=aneg: h.tensor_scalar(out=aneg[:], in0=aneg[:], scalar1=-1.0, scalar2=None, op0=ALU.mult),
               reads=[b_aneg], writes=[b_aneg])
            P["aneg"], P["b_aneg"] = aneg, b_aneg
            idd, b_idd = T_("idd%d" % l, [64, 512], BF16)
            op("dve", lambda h, P=P, idd=idd: h.tensor_tensor(out=idd[:].rearrange("p (a b) -> p a b", a=8),
                                                              in0=ident[0:64, 0:64].unsqueeze(1).broadcast_to([64, 8, 64]),
                                                              in1=P["sbc"][:, 2, :].unsqueeze(2).broadcast_to([64, 8, 64]), op=ALU.mult),
               reads=[b_ident, P["b_sbc"]], writes=[b_idd])
            P["idd"], P["b_idd"] = idd, b_idd
            LP.append(P)

        x, _ = T_("x", [128, KC, TS])
        bx = [Buf() for _ in range(KC)]
        xn, b_xn = T_("xn", [128, KC, TS], BF16)
        Fb = [fw.sbuf("F%d" % i, [128, 4, FWD], F32) for i in range(5)]
        bF = [[Buf() for _ in range(4)] for _ in range(5)]
        Hb = [fw.sbuf("H%d" % i, [128, 4, TS], BF16) for i in range(3)]
        bH = [[Buf() for _ in range(4)] for _ in range(3)]
        VT, b_VT = T_("VT", [64, 8, 512], BF16)
        pTb, b_pTb = T_("pTb", [128, 2, TS], BF16)
        NST, NBF = 1, 4
        stg = [fw.sbuf("stg%d" % i, [128, 2048], F32) for i in range(NST)]
        b_stg = [Buf() for _ in range(NST)]
        b_stgh = [Buf(), Buf()]
        wbf = [fw.sbuf("wbf%d" % i, [128, 2048], BF16) for i in range(NBF)]
        b_wbf = [Buf() for _ in range(NBF)]
        wctr = [0]
        PS = [fw.psum("ps%d" % i, [128, 512], F32) for i in range(6)]
        bPS = [Buf(excl=True) for _ in range(6)]
        PSB = [fw.psum("psb%d" % i, [128, 1024], BF16) for i in range(2)]
        bPSB = [Buf(excl=True) for _ in range(2)]
        pctr = [0]

        pools = {"fg": [0, 1, 2, 3], "bg": [0, 1, 2, 3]}
        pctr_bg = [0]

        def next_ps():
            if fw.in_bg:
                pl = pools["bg"]
                i = pl[pctr_bg[0] % len(pl)]
                pctr_bg[0] += 1
            else:
                pl = pools["fg"]
                i = pl[pctr[0] % len(pl)]
                pctr[0] += 1
            return PS[i], bPS[i]

        rs_a, b_rs_a = T_("rs_a", [128, 512])
        rs_b, b_rs_b = T_("rs_b", [128, 512])
        lrT, b_lrT = T_("lrT", [16, TS])
        KT = [T_("KT%d" % i, [64, 256], BF16) for i in range(1)]
        SC = [T_("SC%d" % i, [64, 512], BF16) for i in range(1)]
        ON = [T_("ON%d" % i, [64, 512], BF16) for i in range(1)]
        sm = [T_("sm%d" % i, [64, 16]) for i in range(1)]
        XT = [T_("XT%d" % i, [64, 512], BF16) for i in range(1)]
        XD = [T_("XD%d" % i, [64, 512], BF16) for i in range(1)]
        XW = [T_("XW%d" % i, [64, 512], BF16) for i in range(1)]
        BK = [T_("BK%d" % i, [64, 256], BF16) for i in range(1)]
        EE = [T_("EE%d" % i, [64, 512]) for i in range(1)]
        SQ = EE
        CD = [T_("CD%d" % i, [8, 512]) for i in range(1)]
        Y1 = [T_("Y1%d" % i, [64, 512]) for i in range(1)]
        Y2 = [T_("Y2%d" % i, [64, 512]) for i in range(1)]
        ZS, b_ZS = VT, b_VT
        DT, b_DT = T_("DT", [64, 8, 8])
        DA, b_DA = T_("DA", [64, 8, 8])
        CUM, b_CUM = T_("CUM", [64, 8, 8])
        ECUM, b_ECUM = T_("ECUM", [64, 8, 8])
        WEND, b_WEND = T_("WEND", [64, 8, 8])
        CUMT, b_CUMT = T_("CUMT", [8, TS])
        NCUMT, b_NCUMT = T_("NCUMT", [8, TS])
        DLB, b_DLB = T_("DLB", [128, 8, 8])

        def mk_state(sk):
            S = []
            for l in range(DEPTH):
                d = {}
                for nm, shape, dt in (("lc", [128, 4, 3], F32), ("lh", [128, 4], F32), ("gla", [64, 4, 128], F32),
                                      ("glab", [64, 4, 128], BF16), ("glab2", [64, 4, 128], BF16), ("sc", [128, 8, 3], F32), ("ssd", [128, 8, 64], F32),
                                      ("ssdb", [128, 8, 64], BF16), ("ssdb2", [128, 8, 64], BF16), ("fc", [128, 24, 2], F32)):
                    t, b = T_("st_%s_%s%d" % (sk, nm, l), shape, dt)
                    d[nm], d["b_" + nm] = t, b
                S.append(d)
            return S

        NCACHE = 144
        wcache = nc.dram_tensor("wcache", [NCACHE, 128, 2048], BF16, kind="Internal").ap()
        b_cache = [Buf() for _ in range(NCACHE)]
        wmode = {"idx": 0, "first": True}

        wkeys = {}

        def wslot(src_ap, a, bcols):
            i = wctr[0]
            wctr[0] += 1
            key = (src_ap.tensor.name, src_ap.offset, a, bcols)
            first = key not in wkeys
            if first:
                wkeys[key] = len(wkeys)
            idx = wkeys[key]
            w_t, w_b = wbf[i % NBF], b_wbf[i % NBF]
            n = a * bcols
            if first:
                s_t = stg[0]
                ah = a // 2 if a >= 2 else a
                parts = [(0, ah)] + ([(ah, a)] if ah < a else [])
                for pi_, (a0, a1) in enumerate(parts):
                    n0, n1 = a0 * bcols, a1 * bcols
                    sb_ = b_stgh[pi_]
                    sv = s_t[:, 1024 * pi_:1024 * pi_ + (n1 - n0)].rearrange("p (a b) -> p a b", a=a1 - a0)
                    dma("w", lambda h, sv=sv, a0=a0, a1=a1: h.dma_start(out=sv, in_=src_ap[:, a0:a1, :]), writes=[sb_])
                    op("act", lambda h, pi_=pi_, n0=n0, n1=n1: h.copy(out=w_t[:, n0:n1], in_=s_t[:, 1024 * pi_:1024 * pi_ + (n1 - n0)]),
                       reads=[sb_], writes=[w_b])
                dma("w", lambda h: h.dma_start(out=wcache[idx, :, 0:n], in_=w_t[:, 0:n]), reads=[w_b], writes=[b_cache[idx]])
            else:
                dma("w", lambda h: h.dma_start(out=w_t[:, 0:n], in_=wcache[idx, :, 0:n]), reads=[b_cache[idx]], writes=[w_b])
            return w_t[:, 0:n].rearrange("p (a b) -> p a b", a=a), w_b

        def win_slot(l, m0, mw):
            return wslot(w_in[l, :, m0:m0 + mw].rearrange("(kc p) m -> p kc m", p=128), KC, mw)

        def tiles_of(T):
            return [(c0, min(512, T - c0)) for c0 in range(0, T, 512)]

        def proj(wv, wb, nk, mblocks, rhs, rhs_bufs, T, evac):
            for mi, (mo, mw) in enumerate(mblocks):
                for (c0, n) in tiles_of(T):
                    ps, pb = next_ps()
                    for k in range(nk):
                        op("pe", lambda h, ps=ps, k=k, mo=mo, mw=mw, c0=c0, n=n: h.matmul(
                            ps[0:mw, 0:n], lhsT=wv[:, k, mo:mo + mw], rhs=rhs[:, k, c0:c0 + n],
                            start=(k == 0), stop=(k == nk - 1)), reads=[wb] + rhs_bufs, writes=[pb])
                    evac(mi, c0, n, ps, pb)

        def rmsnorm(gt, gb, gsel, T, dst, dst_bufs):
            for (c0, n) in tiles_of(T):
                for hf in range(2):
                    op("act", lambda h, c0=c0, n=n, hf=hf: h.activation(out=Hb[1 + hf][:, :, 0:n], in_=x[:, 4 * hf:4 * hf + 4, c0:c0 + n], func=AF.Square),
                       reads=bx[4 * hf:4 * hf + 4], writes=bH[1 + hf])
                ps, pb = next_ps()
                for k in range(KC):
                    op("pe", lambda h, ps=ps, k=k, n=n: h.matmul(ps[:, 0:n], lhsT=onesb[:], rhs=Hb[1 + k // 4][:, k % 4, 0:n],
                                                                  start=(k == 0), stop=(k == KC - 1)),
                       reads=[b_onesb, bH[1 + k // 4][k % 4]], writes=[pb])
                op("act", lambda h, ps=ps, n=n: h.activation(out=rs_a[:, 0:n], in_=ps[:, 0:n], func=AF.Sqrt, scale=1.0 / D, bias=EPS),
                   reads=[pb], writes=[b_rs_a])
                op("dve", lambda h, n=n: h.reciprocal(out=rs_b[:, 0:n], in_=rs_a[:, 0:n]), reads=[b_rs_a], writes=[b_rs_b])
                for k in range(KC):
                    op("dve", lambda h, k=k, c0=c0, n=n: h.scalar_tensor_tensor(
                        out=dst[:, k, c0:c0 + n], in0=x[:, k, c0:c0 + n], scalar=gsel(k), in1=rs_b[:, 0:n],
                        op0=ALU.mult, op1=ALU.mult), reads=[bx[k], gb, b_rs_b], writes=dst_bufs)

        def run_pass(sk, T, col0, S, load_state, store_state, so2=False, store_y=True, ycol0=None, load_x=True, prefetch=None):
            nch = T // 64
            wmode["idx"] = 0
            if load_x:
                dma("io", lambda h: h.dma_start(out=x[:, :, 0:T], in_=xT[sk][:, col0:col0 + T].rearrange("(kc p) t -> p kc t", p=128)),
                    writes=bx)
            if load_state:
                for l in range(DEPTH):
                    for nm in ("lc", "lh", "gla", "sc", "ssd", "fc"):
                        dma("io", lambda h, l=l, nm=nm: h.dma_start(out=S[l][nm][:], in_=st_in[nm][l]), writes=[S[l]["b_" + nm]])
                    op("act", lambda h, l=l: h.copy(out=S[l]["glab"][:], in_=S[l]["gla"][:]), reads=[S[l]["b_gla"]], writes=[S[l]["b_glab"]])
                    op("act", lambda h, l=l: h.copy(out=S[l]["ssdb"][:], in_=S[l]["ssd"][:]), reads=[S[l]["b_ssd"]], writes=[S[l]["b_ssdb"]])

            def layer_body(l, P, Sl, so):
                fc_only = (so == "fc")
                so = (so is True)
                def proj_w(src2d, col0, ncols, mw, evac, U=None):
                    mi_base = 0
                    units = []
                    for s0 in range(0, ncols, 256):
                        sw = min(256, ncols - s0)
                        blocks = [(mo, min(mw, sw - mo)) for mo in range(0, sw, mw)]
                        holder = {}

                        def mk(mi, mo, mwid, first, s0=s0, sw=sw, base=mi_base, holder=holder):
                            def u():
                                if first:
                                    holder["w"] = wslot(src2d[:, col0 + s0:col0 + s0 + sw].rearrange("(kc p) m -> p kc m", p=128), KC, sw)
                                wv, wb = holder["w"]

                                def ev(mi_, c0, n, ps, pb):
                                    evac(base + mi, c0, n, ps, pb)
                                proj(wv, wb, KC, [(mo, mwid)], xn, [b_xn], T, ev)
                            return u
                        for mi, (mo, mwid) in enumerate(blocks):
                            units.append(mk(mi, mo, mwid, mi == 0))
                        mi_base += len(blocks)
                    if U is None:
                        for u in units:
                            u()
                    else:
                        U.extend(units)

                def proj_tok(src2d, col0, dst, dbuf, func):
                    ws = [wslot(src2d[512 * kh:512 * kh + 512, col0:col0 + 512].rearrange("(kc p) m -> p kc m", p=128), 4, 512) for kh in range(2)]
                    for j in range(nch):
                        ps, pb = next_ps()
                        for k in range(KC):
                            wv, wb = ws[k // 4]
                            op("pe", lambda h, ps=ps, k=k, j=j, wv=wv: h.matmul(ps[0:64, 0:512], lhsT=xn[:, k, 64 * j:64 * j + 64], rhs=wv[:, k % 4, :],
                                                                                  start=(k == 0), stop=(k == KC - 1)), reads=[wb, b_xn], writes=[pb])
                        op("act", lambda h, ps=ps, j=j: h.activation(out=dst[:, j, :], in_=ps[0:64, 0:512], func=func),
                           reads=[pb], writes=[dbuf])

                def acc_proj(src2d, r0, nk, Y, Yb, U=None):
                    cols = 2048 // nk
                    units = []
                    for cb in range(0, 1024, cols):
                        holder = {}

                        def mk(mm, first, cb=cb, holder=holder):
                            def u():
                                if first:
                                    holder["w"] = wslot(src2d[r0:r0 + nk * 128, cb:cb + cols].rearrange("(kc p) m -> p kc m", p=128), nk, cols)
                                wv, wb = holder["w"]

                                def ev(mi_, c0, n, ps, pb):
                                    m = cb // 128 + mm
                                    op("dve", lambda h: h.tensor_tensor(out=x[:, m, c0:c0 + n], in0=x[:, m, c0:c0 + n], in1=ps[:, 0:n], op=ALU.add),
                                       reads=[pb, bx[m]], writes=[bx[m]])
                                proj(wv, wb, nk, [(mm * 128, 128)], Y, Yb, T, ev)
                            return u
                        for mm in range(cols // 128):
                            units.append(mk(mm, mm == 0))
                    if U is None:
                        for u in units:
                            u()
                    else:
                        U.extend(units)

                if not so:
                    dma("io", lambda h, l=l: h.dma_start(out=Fb[4][:, 0:2, 0:T], in_=pT[sk][l, :, col0:col0 + T].rearrange("(c p) t -> p c t", p=128)),
                        writes=bF[4])
                    op("pool", lambda h: h.tensor_copy(out=pTb[:, :, 0:T], in_=Fb[4][:, 0:2, 0:T]), reads=bF[4], writes=[b_pTb])

                rmsnorm(P["norms"], P["b_norms"], lambda k, P=P: P["norms"][:, 0, k:k + 1], T, xn, [b_xn])

                AXb, XC, R_, IG, A_ = Fb[0], Fb[1], Fb[2], Fb[3], Fb[4]
                YA = Fb[1][:].rearrange("p a b -> p (a b)")[:, 0:1024].bitcast(BF16).rearrange("p (k t) -> p k t", k=4)

                def lru_units(U):
                    def ev_ax(mi, c0, n, ps, pb):
                        op("act", lambda h: h.copy(out=AXb[:, mi, 3 + c0:3 + c0 + n], in_=ps[:, 0:n]), reads=[pb], writes=[bF[0][mi]])
                    proj_w(w_in[l], 0, 512, 128, ev_ax, U)

                    def u_halo():
                        op("dve", lambda h: h.tensor_copy(out=AXb[:, :, 0:3], in_=Sl["lc"][:]), reads=[Sl["b_lc"]], writes=bF[0])
                    U.append(u_halo)

                    def u_conv():
                        for c in range(4):
                            op("act", lambda h, c=c: h.activation(out=XC[:, c, 0:T], in_=AXb[:, c, 0:T], func=AF.Identity,
                                                                  scale=P["lru"][:, c, 0:1], bias=P["lru"][:, c, 4:5]),
                               reads=[bF[0][c], P["b_lru"]], writes=[bF[1][c]])
                        for j in range(1, 4):
                            for c in range(4):
                                op("dve", lambda h, j=j, c=c: h.scalar_tensor_tensor(out=XC[:, c, 0:T], in0=AXb[:, c, j:j + T],
                                                                                      scalar=P["lru"][:, c, j:j + 1], in1=XC[:, c, 0:T],
                                                                                      op0=ALU.mult, op1=ALU.add),
                                   reads=[bF[0][c], bF[1][c], P["b_lru"]], writes=[bF[1][c]])
                    U.append(u_conv)

                    def u_halo_out():
                        op("pool", lambda h: h.tensor_copy(out=Sl["lc"][:], in_=AXb[:, :, T:T + 3]), reads=bF[0], writes=[Sl["b_lc"]])
                    U.append(u_halo_out)

                    def mk_gate(c, which):
                        def u():
                            wt, bw, dst, bd, bcol = (P["wr"], P["b_wr"], R_, bF[2], 5) if which == 0 else (P["wi"], P["b_wi"], IG, bF[3], 6)
                            for (c0, n) in tiles_of(T):
                                ps, pb = next_ps()
                                op("pe", lambda h, ps=ps, c0=c0, n=n: h.matmul(ps[:, 0:n], lhsT=wt[:, c, :], rhs=XC[:, c, c0:c0 + n],
                                                                                 start=True, stop=True), reads=[bw, bF[1][c]], writes=[pb])
                                op("act", lambda h, ps=ps, c0=c0, n=n: h.activation(out=dst[:, c, c0:c0 + n], in_=ps[:, 0:n], func=AF.Sigmoid,
                                                                                      bias=P["lru"][:, c, bcol:bcol + 1]), reads=[pb, P["b_lru"]], writes=[bd[c]])
                        return u
                    for c in range(4):
                        U.append(mk_gate(c, 0))
                        U.append(mk_gate(c, 1))

                    def u_el():
                        C4 = range(4)
                        for c in C4:
                            op("act", lambda h, c=c: h.activation(out=A_[:, c, 0:T], in_=R_[:, c, 0:T], func=AF.Exp, scale=P["cA"][:, c:c + 1]),
                               reads=[bF[2][c], P["b_cA"]], writes=[bF[4][c]])
                        for c in C4:
                            op("dve", lambda h, c=c: h.tensor_tensor(out=AXb[:, c, 0:T], in0=A_[:, c, 0:T], in1=A_[:, c, 0:T], op=ALU.mult),
                               reads=[bF[4][c]], writes=[bF[0][c]])
                        for c in C4:
                            op("act", lambda h, c=c: h.activation(out=AXb[:, c, 0:T], in_=AXb[:, c, 0:T], func=AF.Sqrt, scale=-1.0, bias=1.0),
                               reads=[bF[0][c]], writes=[bF[0][c]])
                        for c in C4:
                            op("dve", lambda h, c=c: h.tensor_tensor(out=IG[:, c, 0:T], in0=IG[:, c, 0:T], in1=XC[:, c, 0:T], op=ALU.mult),
                               reads=[bF[3][c], bF[1][c]], writes=[bF[3][c]])
                        for c in C4:
                            op("dve", lambda h, c=c: h.tensor_tensor(out=IG[:, c, 0:T], in0=IG[:, c, 0:T], in1=AXb[:, c, 0:T], op=ALU.mult),
                               reads=[bF[3][c], bF[0][c]], writes=[bF[3][c]])
                        for c in C4:
                            op("dve", lambda h, c=c: h.tensor_tensor_scan(out=R_[:, c, 0:T], data0=A_[:, c, 0:T], data1=IG[:, c, 0:T],
                                                                          initial=Sl["lh"][:, c:c + 1], op0=ALU.mult, op1=ALU.add),
                               reads=[bF[4][c], bF[3][c], Sl["b_lh"]], writes=[bF[2][c]])
                        for c in C4:
                            op("dve", lambda h, c=c: h.tensor_copy(out=Sl["lh"][:, c:c + 1], in_=R_[:, c, T - 1:T]), reads=[bF[2][c]], writes=[Sl["b_lh"]])
                    U.append(u_el)

                    def ev_ag(mi, c0, n, ps, pb):
                        op("act", lambda h: h.activation(out=AXb[:, mi, c0:c0 + n], in_=ps[:, 0:n], func=AF.Gelu_apprx_tanh),
                           reads=[pb], writes=[bF[0][mi]])
                        op("dve", lambda h: h.tensor_tensor(out=YA[:, mi, c0:c0 + n], in0=R_[:, mi, c0:c0 + n], in1=AXb[:, mi, c0:c0 + n], op=ALU.mult),
                           reads=[bF[2][mi], bF[0][mi]], writes=bF[1])
                    if not so:
                        proj_w(w_in[l], 512, 512, 128, ev_ag, U)
                        acc_proj(w_out[l], 0, 4, YA, bF[1], U)

                def out_proj(l, r0, Y, Yb):
                    acc_proj(w_out[l], r0, 4, Y, Yb)
                _U = []
                lru_units(_U)
                for _u in _U:
                    _u()
                if so and prefetch is not None:
                    psk, pcol = prefetch
                    dma("io", lambda h: h.dma_start(out=x[:, :, 0:T], in_=xT[psk][:, pcol:pcol + T].rearrange("(kc p) t -> p kc t", p=128)),
                        writes=bx)

                if _STOP[0] < 1:
                    return
                Q, K_, L_, CL, EQ = Fb[0], Fb[1], Fb[2], Fb[3], Fb[4]
                SG, QD, KI = Hb[0], Hb[1], Hb[2]
                def ev_qk(mi, c0, n, ps, pb):
                    dst, db = (Q, bF[0]) if mi < 4 else (K_, bF[1])
                    op("act", lambda h: h.copy(out=dst[0:64, mi % 4, c0:c0 + n], in_=ps[0:64, 0:n]), reads=[pb], writes=[db[mi % 4]])
                if so:
                    proj_w(w_in[l], 1280, 256, 64, lambda mi, c0, n, ps, pb: ev_qk(mi + 4, c0, n, ps, pb))
                else:
                    proj_w(w_in[l], 1024, 512, 64, ev_qk)
                proj_tok(w_in[l], 1536, VT, b_VT, AF.Copy)

                def ev_g(mi, c0, n, ps, pb):
                    rs_, b_rs_ = (rs_a, b_rs_a) if mi % 2 == 0 else (rs_b, b_rs_b)
                    op("act", lambda h: h.activation(out=rs_[:, 0:n], in_=ps[:, 0:n], func=AF.Silu), reads=[pb], writes=[b_rs_])
                    op("dve", lambda h: h.tensor_scalar(out=SG[:, mi, c0:c0 + n], in0=rs_[:, 0:n], scalar1=P["gnorm"][:, mi:mi + 1],
                                                        scalar2=None, op0=ALU.mult), reads=[b_rs_, P["b_gnorm"]], writes=[bH[0][mi]])
                if not so:
                    proj_w(w_in[l], 2048, 512, 128, ev_g)

                def ev_lr(mi, c0, n, ps, pb):
                    op("act", lambda h: h.copy(out=lrT[:, c0:c0 + n], in_=ps[0:16, 0:n]), reads=[pb], writes=[b_lrT])
                proj_w(w_in[l], 2560, 16, 16, ev_lr)
                HS = range(4)
                for hh in HS:
                    for (c0, n) in tiles_of(T):
                        ps, pb = next_ps()
                        op("pe", lambda h, ps=ps, hh=hh, c0=c0, n=n: h.matmul(ps[0:64, 0:n], lhsT=P["wlr"][:, hh * 64:hh * 64 + 64], rhs=lrT[:, c0:c0 + n],
                                                                                start=True, stop=True), reads=[P["b_wlr"], b_lrT], writes=[pb])
                        op("act", lambda h, ps=ps, hh=hh, c0=c0, n=n: h.activation(out=L_[0:64, hh, c0:c0 + n], in_=ps[0:64, 0:n], func=AF.Exp,
                                                                                     scale=-1.0, bias=P["nblr"][:, hh:hh + 1]),
                           reads=[pb, P["b_nblr"]], writes=[bF[2][hh]])
                for hh in HS:
                    op("act", lambda h, hh=hh: h.activation(out=L_[0:64, hh, 0:T], in_=L_[0:64, hh, 0:T], func=AF.Ln, bias=1.0),
                       reads=[bF[2][hh]], writes=[bF[2][hh]])
                for hh in HS:
                    op("dve", lambda h, hh=hh: h.tensor_tensor_scan(out=CL[0:64, hh, 0:T], data0=rmask[:, 0:T], data1=L_[0:64, hh, 0:T],
                                                                     initial=0.0, op0=ALU.mult, op1=ALU.add),
                       reads=[bF[2][hh], b_rmask], writes=[bF[3][hh]])
                for hh in HS:
                    op("act", lambda h, hh=hh: h.activation(out=EQ[0:64, hh, 0:T], in_=CL[0:64, hh, 0:T], func=AF.Exp, scale=-1.0 / 16.0),
                       reads=[bF[3][hh]], writes=[bF[4][hh]])
                for hh in HS:
                    op("act", lambda h, hh=hh: h.activation(out=L_[0:64, hh, 0:T], in_=CL[0:64, hh, 0:T], func=AF.Exp, scale=1.0 / 16.0),
                       reads=[bF[3][hh]], writes=[bF[2][hh]])
                for hh in (HS if not so else []):
                    op("dve", lambda h, hh=hh: h.scalar_tensor_tensor(out=QD[0:64, hh, 0:T], in0=Q[0:64, hh, 0:T], scalar=0.125, in1=EQ[0:64, hh, 0:T],
                                                                       op0=ALU.mult, op1=ALU.mult), reads=[bF[0][hh], bF[4][hh]], writes=[bH[1][hh]])
                for hh in HS:
                    op("dve", lambda h, hh=hh: h.tensor_tensor(out=KI[0:64, hh, 0:T], in0=K_[0:64, hh, 0:T], in1=L_[0:64, hh, 0:T], op=ALU.mult),
                       reads=[bF[1][hh], bF[2][hh]], writes=[bH[2][hh]])
                def ssd_preA_units(U):
                    XB0, XB1, PR0, PR1 = Fb[0], Fb[1], Fb[2], Fb[3]

                    def ev_xb(mi, c0, n, ps, pb):
                        XBh, bXB = (XB0, bF[0]) if mi < 4 else (XB1, bF[1])
                        op("act", lambda h: h.copy(out=XBh[:, mi % 4, 3 + c0:3 + c0 + n], in_=ps[:, 0:n]), reads=[pb], writes=[bXB[mi % 4]])
                    proj_w(w_in[l], 3088, 1024, 128, ev_xb, U)
                    for half in range(2):
                        XBh, bXB = (XB0, bF[0]) if half == 0 else (XB1, bF[1])
                        PRh, bPR = (PR0, bF[2]) if half == 0 else (PR1, bF[3])

                        def u_halo(XBh=XBh, bXB=bXB, half=half):
                            op("dve", lambda h: h.tensor_copy(out=XBh[:, :, 0:3], in_=Sl["sc"][:, 4 * half:4 * half + 4, :]),
                               reads=[Sl["b_sc"]], writes=bXB)
                        U.append(u_halo)

                        def u_tap0(XBh=XBh, bXB=bXB, PRh=PRh, bPR=bPR, half=half):
                            for c in range(4):
                                cc = 4 * half + c
                                op("act", lambda h, c=c, cc=cc: h.activation(out=PRh[:, c, 0:T], in_=XBh[:, c, 0:T], func=AF.Identity,
                                                                             scale=P["scw"][:, cc, 0:1], bias=P["scw"][:, cc, 4:5]),
                                   reads=[bXB[c], P["b_scw"]], writes=[bPR[c]])
                        U.append(u_tap0)

                        def mk_tap(j, XBh=XBh, bXB=bXB, PRh=PRh, bPR=bPR, half=half):
                            def u():
                                for c in range(4):
                                    cc = 4 * half + c
                                    op("dve", lambda h, c=c, cc=cc: h.scalar_tensor_tensor(
                                        out=PRh[:, c, 0:T], in0=XBh[:, c, j:j + T], scalar=P["scw"][:, cc, j:j + 1], in1=PRh[:, c, 0:T],
                                        op0=ALU.mult, op1=ALU.add), reads=[bXB[c], bPR[c], P["b_scw"]], writes=[bPR[c]])
                            return u
                        for j in range(1, 4):
                            U.append(mk_tap(j))

                        def u_save(XBh=XBh, bXB=bXB, half=half):
                            op("pool", lambda h: h.tensor_copy(out=Sl["sc"][:, 4 * half:4 * half + 4, :], in_=XBh[:, :, T:T + 3]),
                               reads=bXB, writes=[Sl["b_sc"]])
                        U.append(u_save)

                YG, bYG = Hb[0], bH[0]
                def u_ssd_dt():
                    wv, wb = win_slot(l, 4112, 8)
                    ps, pb = next_ps()
                    for j in range(nch):
                        for k in range(KC):
                            op("pe", lambda h, ps=ps, k=k, j=j, wv=wv: h.matmul(ps[0:64, 8 * j:8 * j + 8], lhsT=xn[:, k, 64 * j:64 * j + 64], rhs=wv[:, k, 0:8],
                                                                                  start=(k == 0), stop=(k == KC - 1)), reads=[wb, b_xn], writes=[pb])
                    nn = nch * 8
                    op("dve", lambda h, ps=ps: h.tensor_tensor(out=DT[:, 0:nch, :], in0=ps[0:64, 0:nn].rearrange("p (a b) -> p a b", b=8),
                                                               in1=P["sbc"][:, 0:1, :].broadcast_to([64, nch, 8]), op=ALU.add),
                       reads=[pb, P["b_sbc"]], writes=[b_DT])
                    op("act", lambda h: h.activation(out=DT[:, 0:nch, :], in_=DT[:, 0:nch, :], func=AF.Softplus), reads=[b_DT], writes=[b_DT])
                    op("dve", lambda h: h.tensor_tensor(out=DA[:, 0:nch, :], in0=DT[:, 0:nch, :], in1=P["aneg"][:].unsqueeze(1).broadcast_to([64, nch, 8]), op=ALU.mult),
                       reads=[b_DT, P["b_aneg"]], writes=[b_DA])
                    ps, pb = next_ps()
                    ps2, pb2 = next_ps()
                    for j in range(nch):
                        op("pe", lambda h, ps=ps, j=j: h.matmul(ps[0:64, 8 * j:8 * j + 8], lhsT=maskT[:], rhs=DA[:, j, :], start=True, stop=True),
                           reads=[b_mask, b_DA], writes=[pb])
                        op("pe", lambda h, ps2=ps2, j=j: h.matmul(ps2[0:64, 8 * j:8 * j + 8], lhsT=Umat[:], rhs=DA[:, j, :], start=True, stop=True),
                           reads=[b_U, b_DA], writes=[pb2])
                    op("act", lambda h, ps=ps: h.copy(out=CUM[:, 0:nch, :], in_=ps[0:64, 0:nn].rearrange("p (a b) -> p a b", b=8)), reads=[pb], writes=[b_CUM])
                    op("act", lambda h, ps=ps: h.activation(out=ECUM[:, 0:nch, :], in_=ps[0:64, 0:nn].rearrange("p (a b) -> p a b", b=8), func=AF.Exp),
                       reads=[pb], writes=[b_ECUM])
                    op("act", lambda h, ps2=ps2: h.activation(out=WEND[:, 0:nch, :], in_=ps2[0:64, 0:nn].rearrange("p (a b) -> p a b", b=8), func=AF.Exp),
                       reads=[pb2], writes=[b_WEND])
                    op("dve", lambda h: h.tensor_tensor(out=WEND[:, 0:nch, :], in0=WEND[:, 0:nch, :], in1=DT[:, 0:nch, :], op=ALU.mult),
                       reads=[b_WEND, b_DT], writes=[b_WEND])
                    for (c0, n) in (tiles_of(T) if not so else []):
                        ps, pb = next_ps()
                        for jj in range(n // 64):
                            j = c0 // 64 + jj
                            op("pe", lambda h, ps=ps, j=j, jj=jj: h.matmul(ps[0:8, 64 * jj:64 * jj + 64], lhsT=DA[:, j, :], rhs=maskT[:], start=True, stop=True),
                               reads=[b_mask, b_DA], writes=[pb])
                        op("act", lambda h, ps=ps, c0=c0, n=n: h.copy(out=CUMT[:, c0:c0 + n], in_=ps[0:8, 0:n]), reads=[pb], writes=[b_CUMT])
                        op("dve", lambda h, ps=ps, c0=c0, n=n: h.tensor_scalar(out=NCUMT[:, c0:c0 + n], in0=ps[0:8, 0:n], scalar1=-1.0, scalar2=None, op0=ALU.mult),
                           reads=[pb], writes=[b_NCUMT])
                    ps, pb = next_ps()
                    op("pe", lambda h, ps=ps: h.matmul(ps[:, 0:nn], lhsT=sel63[:], rhs=ECUM[:, 0:nch, :].rearrange("p a b -> p (a b)"), start=True, stop=True),
                       reads=[b_sel, b_ECUM], writes=[pb])
                    op("act", lambda h, ps=ps: h.copy(out=DLB[:, 0:nch, :], in_=ps[:, 0:nn].rearrange("p (a b) -> p a b", b=8)), reads=[pb], writes=[b_DLB])

                Ubg = []
                ssd_preA_units(Ubg)
                Ubg.append(u_ssd_dt)
                fw.bg = Ubg
                fw.bg_ctr = 0
                fw.bg_every = 5 if so else 10
                pools["fg"], pools["bg"] = [0, 1], [0, 1]
                GB = [(Sl["glab"], Sl["b_glab"]), (Sl["glab2"], Sl["b_glab2"])]

                def gla_p1(j):
                        cs = slice(64 * j, 64 * j + 64)
                        pq = 0
                        kt, b_kt = KT[pq]
                        sc, b_sc = SC[pq]
                        sq, b_sq = SQ[pq]
                        on, b_on = ON[pq]
                        smt, b_sm = sm[pq]
                        for hh in range(4):
                            op("pe", lambda h, hh=hh, cs=cs: h.transpose(out=PSB[0][0:64, hh * 64:hh * 64 + 64], in_=KI[0:64, hh, cs], identity=identb[0:64, 0:64]),
                               reads=[bH[2][hh], b_identb], writes=[bPSB[0]])
                        op("act", lambda h, kt=kt: h.copy(out=kt[:], in_=PSB[0][0:64, 0:256]), reads=[bPSB[0]], writes=[b_kt])
                        if not so:
                            p_s, b_ps = PS[5], bPS[5]
                            for hh in range(4):
                                op("pe", lambda h, hh=hh, cs=cs, p_s=p_s: h.matmul(p_s[0:64, hh * 64:hh * 64 + 64], lhsT=KI[0:64, hh, cs], rhs=QD[0:64, hh, cs],
                                                                                     start=True, stop=True), reads=[bH[2][hh], bH[1][hh]], writes=[b_ps])
                            op("dve", lambda h, p_s=p_s, sc=sc: h.tensor_tensor(out=sc[:, 0:256].rearrange("p (a b) -> p a b", a=4),
                                                                                 in0=p_s[0:64, 0:256].rearrange("p (a b) -> p a b", a=4),
                                                                                 in1=maskT[:].unsqueeze(1).broadcast_to([64, 4, 64]), op=ALU.mult),
                               reads=[b_ps, b_mask], writes=[b_sc])
                            p_o, b_po = PS[2 + j % 2], bPS[2 + j % 2]
                            for hh in range(4):
                                op("pe", lambda h, hh=hh, j=j, p_o=p_o, sc=sc: h.matmul(p_o[0:64, hh * 128:hh * 128 + 128], lhsT=sc[:, hh * 64:hh * 64 + 64],
                                                                                          rhs=VT[:, j, hh * 128:hh * 128 + 128], start=(hh == 0), stop=False, skip_group_check=True),
                                   reads=[b_sc, b_VT], writes=[b_po])
                        p_d, b_pd = PS[4], bPS[4]
                        for hh in range(4):
                            op("pe", lambda h, hh=hh, j=j, p_d=p_d, kt=kt: h.matmul(p_d[0:64, hh * 128:hh * 128 + 128], lhsT=kt[:, hh * 64:hh * 64 + 64],
                                                                                      rhs=VT[:, j, hh * 128:hh * 128 + 128], start=True, stop=True),
                               reads=[b_kt, b_VT], writes=[b_pd])
                        op("dve", lambda h, p_d=p_d: h.tensor_tensor(out=Sl["gla"][:], in0=Sl["gla"][:], in1=p_d[0:64, :].rearrange("p (a b) -> p a b", a=4), op=ALU.add),
                           reads=[b_pd, Sl["b_gla"]], writes=[Sl["b_gla"]])
                        op("dve", lambda h, j=j: h.tensor_tensor(out=Sl["gla"][:], in0=Sl["gla"][:],
                                                                  in1=EQ[0:64, :, 64 * j + 63:64 * j + 64].broadcast_to([64, 4, 128]), op=ALU.mult),
                           reads=bF[4] + [Sl["b_gla"]], writes=[Sl["b_gla"]])
                        op("act", lambda h, j=j: h.copy(out=GB[(j + 1) % 2][0][:], in_=Sl["gla"][:]), reads=[Sl["b_gla"]], writes=[GB[(j + 1) % 2][1]])

                def gla_p2(j):
                        cs = slice(64 * j, 64 * j + 64)
                        pq = 0
                        kt, b_kt = KT[pq]
                        sc, b_sc = SC[pq]
                        sq, b_sq = SQ[pq]
                        on, b_on = ON[pq]
                        smt, b_sm = sm[pq]
                        p_o, b_po = PS[2 + j % 2], bPS[2 + j % 2]
                        for hh in range(4):
                            op("pe", lambda h, hh=hh, cs=cs, p_o=p_o: h.matmul(p_o[0:64, hh * 128:hh * 128 + 128], lhsT=QD[0:64, hh, cs],
                                                                                 rhs=GB[j % 2][0][:, hh, :], start=False, stop=True, skip_group_check=True),
                               reads=[bH[1][hh], GB[j % 2][1]], writes=[b_po])
                        op("act", lambda h, p_o=p_o, sq=sq: h.activation(out=sq[:], in_=p_o[0:64, :], func=AF.Square), reads=[b_po], writes=[b_sq])
                        op("dve", lambda h, sq=sq, smt=smt: h.tensor_reduce(out=smt[:, 0:4], in_=sq[:].rearrange("p (a b) -> p a b", a=4), axis=AX.X, op=ALU.add),
                           reads=[b_sq], writes=[b_sm])
                        op("act", lambda h, smt=smt: h.activation(out=smt[:, 4:8], in_=smt[:, 0:4], func=AF.Ln, scale=1.0 / 128.0, bias=EPS),
                           reads=[b_sm], writes=[b_sm])
                        op("act", lambda h, smt=smt: h.activation(out=smt[:, 8:12], in_=smt[:, 4:8], func=AF.Exp, scale=-0.5), reads=[b_sm], writes=[b_sm])
                        op("dve", lambda h, p_o=p_o, on=on, smt=smt: h.tensor_tensor(out=on[:].rearrange("p (a b) -> p a b", a=4),
                                                                                      in0=p_o[0:64, :].rearrange("p (a b) -> p a b", a=4),
                                                                                      in1=smt[:, 8:12].unsqueeze(2).broadcast_to([64, 4, 128]), op=ALU.mult),
                           reads=[b_po, b_sm], writes=[b_on])
                        for hh in range(4):
                            op("pe", lambda h, hh=hh, on=on: h.transpose(out=PSB[1][:, hh * 64:hh * 64 + 64], in_=on[:, hh * 128:hh * 128 + 128], identity=identb[0:64, 0:64]),
                               reads=[b_on, b_identb], writes=[bPSB[1]])
                        op("dve", lambda h, cs=cs: h.tensor_tensor(out=YG[:, :, cs], in0=PSB[1][:, 0:256].rearrange("p (a b) -> p a b", a=4),
                                                                    in1=SG[:, :, cs], op=ALU.mult), reads=[bPSB[1]] + bH[0], writes=bH[0])

                if so:
                    for j in range(nch):
                        gla_p1(j)
                else:
                    fw.begin_capture()
                    gla_p1(0)
                    fw.issue(fw.end_capture())
                    for j in range(nch):
                        fw.begin_capture()
                        gla_p2(j)
                        B_ = fw.end_capture()
                        A_ops = []
                        if j + 1 < nch:
                            fw.begin_capture()
                            gla_p1(j + 1)
                            A_ops = fw.end_capture()
                        fw.interleave(B_, A_ops)
                fw.run_bg()
                pools["fg"], pools["bg"] = [0, 1, 2, 3], [0, 1, 2, 3]
                if not so:
                    out_proj(l, 512, YG, bYG)

                if _STOP[0] < 2:
                    return
                XB0, XB1, PR0, PR1 = Fb[0], Fb[1], Fb[2], Fb[3]
                XS, BC, YC = Hb[0], Hb[1], Hb[2]
                for half in range(2):
                    PRh, bPR = (PR0, bF[2]) if half == 0 else (PR1, bF[3])
                    for c in range(4):
                        dstH, bdH = (XS, bH[0]) if half == 0 else (BC, bH[1])
                        op("act", lambda h, c=c, PRh=PRh, dstH=dstH: h.activation(out=dstH[:, c, 0:T], in_=PRh[:, c, 0:T], func=AF.Silu),
                           reads=[bPR[c]], writes=[bdH[c]])
                if not so:
                    proj_tok(w_in[l], 2576, ZS, b_ZS, AF.Silu)

                fw.bg = []
                pools["fg"], pools["bg"] = [0], [0]
                SB = [(Sl["ssdb"], Sl["b_ssdb"]), (Sl["ssdb2"], Sl["b_ssdb2"])]

                def ssd_p1(j):
                        cs = slice(64 * j, 64 * j + 64)
                        pq = 0
                        xt, b_xt = XT[pq]
                        xd, b_xd = XD[pq]
                        xw, b_xw = XW[pq]
                        bk, b_bk = BK[pq]
                        ee, b_ee = EE[pq]
                        cd, b_cd = CD[pq]
                        y1, b_y1 = Y1[pq]
                        y2, b_y2 = Y2[pq]
                        sc, b_sc = SC[pq]
                        on, b_on = ON[pq]
                        smt, b_sm = sm[pq]
                        for c in range(4):
                            op("pe", lambda h, c=c, cs=cs: h.transpose(out=PSB[0][0:64, c * 128:c * 128 + 128], in_=XS[:, c, cs], identity=identb[:]),
                               reads=[bH[0][c], b_identb], writes=[bPSB[0]])
                        op("act", lambda h, xt=xt: h.copy(out=xt[:], in_=PSB[0][0:64, 0:512]), reads=[bPSB[0]], writes=[b_xt])
                        for c in range(2):
                            op("pe", lambda h, c=c, cs=cs: h.transpose(out=PSB[0][0:64, 512 + c * 128:512 + c * 128 + 128], in_=BC[:, c, cs], identity=identb[:]),
                               reads=[bH[1][c], b_identb], writes=[bPSB[0]])
                        op("act", lambda h, bk=bk: h.copy(out=bk[:], in_=PSB[0][0:64, 512:768]), reads=[bPSB[0]], writes=[b_bk])
                        if not so:
                            op("pool", lambda h, xt=xt, xd=xd, j=j: h.tensor_tensor(out=xd[:].rearrange("p (a b) -> p a b", a=8), in0=xt[:].rearrange("p (a b) -> p a b", a=8),
                                                                                    in1=DT[:, j, :].unsqueeze(2).broadcast_to([64, 8, 64]), op=ALU.mult),
                               reads=[b_xt, b_DT], writes=[b_xd])
                        op("pool", lambda h, xt=xt, xw=xw, j=j: h.tensor_tensor(out=xw[:].rearrange("p (a b) -> p a b", a=8), in0=xt[:].rearrange("p (a b) -> p a b", a=8),
                                                                                in1=WEND[:, j, :].unsqueeze(2).broadcast_to([64, 8, 64]), op=ALU.mult),
                           reads=[b_xt, b_WEND], writes=[b_xw])
                        if not so:
                            p_g, b_pg = PS[5], bPS[5]
                            for g in range(2):
                                op("pe", lambda h, g=g, cs=cs, p_g=p_g: h.matmul(p_g[0:64, g * 64:g * 64 + 64], lhsT=BC[:, g, cs], rhs=BC[:, 2 + g, cs], start=True, stop=True),
                                   reads=[bH[1][g], bH[1][2 + g]], writes=[b_pg])
                            op("dve", lambda h, cd=cd, cs=cs: h.tensor_tensor(out=cd[:].rearrange("p (a b) -> p a b", a=8), in0=diag[:].rearrange("p (a b) -> p a b", a=8),
                                                                               in1=CUMT[:, cs].unsqueeze(1).broadcast_to([8, 8, 64]), op=ALU.mult),
                               reads=[b_diag, b_CUMT], writes=[b_cd])
                            p_e, b_pe = PS[4], bPS[4]
                            op("pe", lambda h, p_e=p_e, cd=cd: h.matmul(p_e[0:64, :], lhsT=ones8[:], rhs=cd[:], start=True, stop=False), reads=[b_ones8, b_cd], writes=[b_pe])
                            op("pe", lambda h, p_e=p_e, cs=cs: h.matmul(p_e[0:64, :], lhsT=NCUMT[:, cs], rhs=diag[:], start=False, stop=False), reads=[b_NCUMT, b_diag], writes=[b_pe])
                            op("pe", lambda h, p_e=p_e: h.matmul(p_e[0:64, :], lhsT=identb[0:64, 0:64], rhs=mnegb[:], start=False, stop=True), reads=[b_identb, b_mnegb], writes=[b_pe])
                            op("act", lambda h, p_e=p_e, ee=ee: h.activation(out=ee[:], in_=p_e[0:64, :], func=AF.Exp), reads=[b_pe], writes=[b_ee])
                            op("dve", lambda h, ee=ee, sc=sc, p_g=p_g: h.tensor_tensor(out=sc[:].rearrange("p (g a b) -> p g a b", g=2, a=4),
                                                                                        in0=ee[:].rearrange("p (g a b) -> p g a b", g=2, a=4),
                                                                                        in1=p_g[0:64, 0:128].rearrange("p (g b) -> p g b", g=2).unsqueeze(2).broadcast_to([64, 2, 4, 64]),
                                                                                        op=ALU.mult), reads=[b_ee, b_pg], writes=[b_sc])
                            p_y, b_py = PS[2 + j % 2], bPS[2 + j % 2]
                            for hh in range(8):
                                op("pe", lambda h, hh=hh, p_y=p_y, sc=sc, xd=xd: h.matmul(p_y[0:64, hh * 64:hh * 64 + 64], lhsT=sc[:, hh * 64:hh * 64 + 64],
                                                                                            rhs=xd[:, hh * 64:hh * 64 + 64], start=True, stop=False),
                                   reads=[b_sc, b_xd], writes=[b_py])
                                op("pe", lambda h, hh=hh, p_y=p_y, xt=xt: h.matmul(p_y[0:64, hh * 64:hh * 64 + 64], lhsT=P["idd"][:, hh * 64:hh * 64 + 64],
                                                                                     rhs=xt[:, hh * 64:hh * 64 + 64], start=False, stop=True),
                                   reads=[P["b_idd"], b_xt], writes=[b_py])
                        p_d, b_pd = PS[5], bPS[5]
                        for g in range(2):
                            op("pe", lambda h, g=g, p_d=p_d, bk=bk, xw=xw: h.matmul(p_d[:, g * 256:g * 256 + 256], lhsT=bk[:, g * 128:g * 128 + 128],
                                                                                      rhs=xw[:, g * 256:g * 256 + 256], start=True, stop=True),
                               reads=[b_bk, b_xw], writes=[b_pd])
                        op("pool", lambda h, j=j: h.tensor_tensor(out=Sl["ssd"][:], in0=Sl["ssd"][:], in1=DLB[:, j, :].unsqueeze(2).broadcast_to([128, 8, 64]), op=ALU.mult),
                           reads=[Sl["b_ssd"], b_DLB], writes=[Sl["b_ssd"]])
                        op("dve", lambda h, p_d=p_d: h.tensor_tensor(out=Sl["ssd"][:], in0=Sl["ssd"][:], in1=p_d[:, :].rearrange("p (a b) -> p a b", a=8), op=ALU.add),
                           reads=[Sl["b_ssd"], b_pd], writes=[Sl["b_ssd"]])
                        op("act", lambda h, j=j: h.copy(out=SB[(j + 1) % 2][0][:], in_=Sl["ssd"][:]), reads=[Sl["b_ssd"]], writes=[SB[(j + 1) % 2][1]])

                def ssd_p2(j):
                        cs = slice(64 * j, 64 * j + 64)
                        pq = 0
                        xt, b_xt = XT[pq]
                        xd, b_xd = XD[pq]
                        xw, b_xw = XW[pq]
                        bk, b_bk = BK[pq]
                        ee, b_ee = EE[pq]
                        cd, b_cd = CD[pq]
                        y1, b_y1 = Y1[pq]
                        y2, b_y2 = Y2[pq]
                        sc, b_sc = SC[pq]
                        on, b_on = ON[pq]
                        smt, b_sm = sm[pq]
                        p_i, b_pi = PS[1], bPS[1]
                        p_y, b_py = PS[2 + j % 2], bPS[2 + j % 2]
                        for g in range(2):
                            op("pe", lambda h, g=g, cs=cs, p_i=p_i: h.matmul(p_i[0:64, g * 256:g * 256 + 256], lhsT=BC[:, 2 + g, cs],
                                                                               rhs=SB[j % 2][0][:, 4 * g:4 * g + 4, :].rearrange("p a b -> p (a b)"), start=True, stop=True),
                               reads=[bH[1][2 + g], SB[j % 2][1]], writes=[b_pi])
                        op("dve", lambda h, p_i=p_i, y1=y1, j=j: h.tensor_tensor(out=y1[:].rearrange("p (a b) -> p a b", a=8), in0=p_i[0:64, :].rearrange("p (a b) -> p a b", a=8),
                                                                                  in1=ECUM[:, j, :].unsqueeze(2).broadcast_to([64, 8, 64]), op=ALU.mult),
                           reads=[b_pi, b_ECUM], writes=[b_y1])
                        op("dve", lambda h, p_y=p_y, y1=y1: h.tensor_tensor(out=y1[:], in0=y1[:], in1=p_y[0:64, :], op=ALU.add), reads=[b_py, b_y1], writes=[b_y1])
                        op("dve", lambda h, y1=y1, j=j: h.tensor_tensor(out=y1[:], in0=y1[:], in1=ZS[:, j, :], op=ALU.mult), reads=[b_y1, b_ZS], writes=[b_y1])
                        op("act", lambda h, y1=y1, y2=y2, smt=smt: h.activation(out=y2[:], in_=y1[:], func=AF.Square, accum_out=smt[:, 12:13]),
                           reads=[b_y1], writes=[b_y2, b_sm])
                        op("act", lambda h, smt=smt: h.activation(out=smt[:, 13:14], in_=smt[:, 12:13], func=AF.Ln, scale=1.0 / 512.0, bias=EPS), reads=[b_sm], writes=[b_sm])
                        op("act", lambda h, smt=smt: h.activation(out=smt[:, 14:15], in_=smt[:, 13:14], func=AF.Exp, scale=-0.5), reads=[b_sm], writes=[b_sm])
                        op("act", lambda h, y1=y1, on=on, smt=smt: h.activation(out=on[:], in_=y1[:], func=AF.Identity, scale=smt[:, 14:15]),
                           reads=[b_y1, b_sm], writes=[b_on])
                        for c in range(4):
                            op("pe", lambda h, c=c, on=on: h.transpose(out=PSB[1][:, c * 64:c * 64 + 64], in_=on[:, c * 128:c * 128 + 128], identity=identb[0:64, 0:64]),
                               reads=[b_on, b_identb], writes=[bPSB[1]])
                        op("dve", lambda h, cs=cs: h.tensor_tensor(out=YC[:, :, cs], in0=PSB[1][:, 0:256].rearrange("p (a b) -> p a b", a=4),
                                                                    in1=P["snorm"][:].unsqueeze(2).broadcast_to([128, 4, 64]), op=ALU.mult),
                           reads=[bPSB[1], P["b_snorm"]], writes=bH[2])

                if so:
                    for j in range(nch):
                        ssd_p1(j)
                else:
                    fw.begin_capture()
                    ssd_p1(0)
                    fw.issue(fw.end_capture())
                    for j in range(nch):
                        fw.begin_capture()
                        ssd_p2(j)
                        B_ = fw.end_capture()
                        A_ops = []
                        if j + 1 < nch:
                            fw.begin_capture()
                            ssd_p1(j + 1)
                            A_ops = fw.end_capture()
                        fw.interleave(B_, A_ops)
                fw.run_bg()
                pools["fg"], pools["bg"] = [0, 1, 2, 3], [0, 1, 2, 3]
                if not so:
                    out_proj(l, 1024, YC, bH[2])

                if _STOP[0] < 3 or so:
                    return
                rmsnorm(P["norms"], P["b_norms"], lambda k, P=P: P["norms"][:, 1, k:k + 1], T, xn, [b_xn])
                GP, GC = Fb[0], Fb[1]
                HHv = [Fb[2 + q][:].rearrange("p a b -> p (a b)")[:, 0:2048].bitcast(BF16).rearrange("p (k t) -> p k t", k=8) for q in range(3)]
                for fs in range(12):
                    f0 = fs * 256
                    hq, ho = fs // 4, 2 * (fs % 4)

                    def ev_gp(mi, c0, n, ps, pb):
                        op("act", lambda h: h.copy(out=GP[:, mi, 2 + c0:2 + c0 + n], in_=ps[:, 0:n]), reads=[pb], writes=[bF[0][mi]])
                    proj_w(w_gate[l], f0, 256, 128, ev_gp)
                    op("dve", lambda h, fs=fs: h.tensor_copy(out=GP[:, 0:2, 0:2], in_=Sl["fc"][:, 2 * fs:2 * fs + 2, :]), reads=[Sl["b_fc"]], writes=bF[0][0:2])
                    if fc_only:
                        op("pool", lambda h, fs=fs: h.tensor_copy(out=Sl["fc"][:, 2 * fs:2 * fs + 2, :], in_=GP[:, 0:2, T:T + 2]), reads=bF[0][0:2], writes=[Sl["b_fc"]])
                        continue
                    for c in range(2):
                        cc = 2 * fs + c
                        op("act", lambda h, c=c, cc=cc: h.activation(out=GC[:, c, 0:T], in_=GP[:, c, 0:T], func=AF.Identity,
                                                                     scale=P["fcw"][:, cc, 0:1], bias=P["fcw"][:, cc, 3:4]),
                           reads=[bF[0][c], P["b_fcw"]], writes=[bF[1][c]])
                    for j in range(1, 3):
                        for c in range(2):
                            cc = 2 * fs + c
                            op("dve", lambda h, c=c, cc=cc, j=j: h.scalar_tensor_tensor(out=GC[:, c, 0:T], in0=GP[:, c, j:j + T], scalar=P["fcw"][:, cc, j:j + 1],
                                                                                         in1=GC[:, c, 0:T], op0=ALU.mult, op1=ALU.add),
                               reads=[bF[0][c], bF[1][c], P["b_fcw"]], writes=[bF[1][c]])
                    for c in range(2):
                        op("act", lambda h, c=c: h.activation(out=GC[:, c, 0:T], in_=GC[:, c, 0:T], func=AF.Gelu_apprx_tanh), reads=[bF[1][c]], writes=[bF[1][c]])
                    op("pool", lambda h, fs=fs: h.tensor_copy(out=Sl["fc"][:, 2 * fs:2 * fs + 2, :], in_=GP[:, 0:2, T:T + 2]), reads=bF[0][0:2], writes=[Sl["b_fc"]])

                    def ev_up(mi, c0, n, ps, pb, hq=hq, ho=ho):
                        op("dve", lambda h: h.tensor_tensor(out=HHv[hq][:, ho + mi, c0:c0 + n], in0=GC[:, mi, c0:c0 + n], in1=ps[:, 0:n], op=ALU.mult),
                           reads=[pb, bF[1][mi]], writes=bF[2 + hq])
                    proj_w(w_up[l], f0, 256, 128, ev_up)
                if fc_only:
                    return
                hh_bufs = bF[2] + bF[3] + bF[4]
                for m in range(8):
                    ps, pb = next_ps()
                    for half in range(2):
                        wv, wb = wslot(w_down[l][1536 * half:1536 * half + 1536, m * 128:m * 128 + 128].rearrange("(kc p) m -> p kc m", p=128), 12, 128)
                        for kk in range(12):
                            kc = 12 * half + kk
                            op("pe", lambda h, ps=ps, wv=wv, kk=kk, kc=kc: h.matmul(ps[:, 0:T], lhsT=wv[:, kk, :], rhs=HHv[kc // 8][:, kc % 8, 0:T],
                                                                                     start=(kc == 0), stop=(kc == 23)), reads=[wb] + hh_bufs, writes=[pb])
                    op("dve", lambda h, ps=ps, m=m: h.tensor_tensor(out=x[:, m, 0:T], in0=x[:, m, 0:T], in1=ps[:, 0:T], op=ALU.add),
                       reads=[pb, bx[m]], writes=[bx[m]])

                if _STOP[0] < 4:
                    return
                rmsnorm(P["norms"], P["b_norms"], lambda k, P=P: P["norms"][:, 2, k:k + 1], T, xn, [b_xn])
                def ev_pg(mi, c0, n, ps, pb):
                    Gt, bG = (Fb[0], bF[0]) if mi < 4 else (Fb[1], bF[1])
                    op("act", lambda h: h.activation(out=Gt[:, mi % 4, c0:c0 + n], in_=ps[:, 0:n], func=AF.Sigmoid), reads=[pb], writes=[bG[mi % 4]])
                proj_w(ple_wg[l], 0, 1024, 128, ev_pg)
                wv, wb = wslot(ple_wp[l].rearrange("(kc p) m -> p kc m", p=128), 2, 1024)

                def ev_pp(mi, c0, n, ps, pb):
                    Gt, bG = (Fb[0], bF[0]) if mi < 4 else (Fb[1], bF[1])
                    rs_, b_rs_ = (rs_a, b_rs_a) if mi % 2 == 0 else (rs_b, b_rs_b)
                    op("dve", lambda h: h.tensor_tensor(out=rs_[:, 0:n], in0=Gt[:, mi % 4, c0:c0 + n], in1=ps[:, 0:n], op=ALU.mult),
                       reads=[pb, bG[mi % 4]], writes=[b_rs_])
                    op("dve", lambda h: h.tensor_tensor(out=x[:, mi, c0:c0 + n], in0=x[:, mi, c0:c0 + n], in1=rs_[:, 0:n], op=ALU.add),
                       reads=[b_rs_, bx[mi]], writes=[bx[mi]])
                proj(wv, wb, 2, [(m * 128, 128) for m in range(8)], pTb, [b_pTb], T, ev_pp)

            for l in range(DEPTH):
                layer_body(l, LP[l], S[l], so2 if l == DEPTH - 1 else False)

            if store_y:
                YF = [Fb[0], Fb[1]]
                yc0 = col0 if ycol0 is None else ycol0

                class _Dst:
                    def __getitem__(self, idx):
                        p, k, c = idx
                        return YF[k // 4][p, k % 4, c]
                rmsnorm(nfinal, b_nfinal, lambda k: nfinal[:, k:k + 1], T, _Dst(), bF[0] + bF[1])
                for half in range(2):
                    dma("io", lambda h, half=half: h.dma_start(
                        out=yT[sk][512 * half:512 * half + 512, yc0:yc0 + T].rearrange("(kc p) t -> p kc t", p=128),
                        in_=YF[half][:, :, 0:T]), reads=bF[half])
            if store_state:
                for l in range(DEPTH):
                    for nm in ("lc", "lh", "gla", "sc", "ssd", "fc"):
                        dma("io", lambda h, l=l, nm=nm: h.dma_start(out=st_out[sk][nm][l], in_=S[l][nm][:]), reads=[S[l]["b_" + nm]])
            wmode["first"] = False

        Sp = mk_state("st")
        if n_sup > 0:
            for l in range(DEPTH):
                for nm in ("lc", "lh", "gla", "glab", "glab2", "sc", "ssd", "ssdb", "ssdb2", "fc"):
                    op("pool", lambda h, l=l, nm=nm: h.memset(Sp[l][nm][:], 0.0), writes=[Sp[l]["b_" + nm]])
            SEGT = NSUP // 4
            pref_done = [False]
            for s in range(n_sup):
                slot = s // SEGT
                last_slot = (slot == 3) or (n_sup < NSUP)
                so2 = (not last_slot)
                if so2 and s == 3 * SEGT - 1:
                    so2 = "fc"
                can_pref = (so2 is True) and (s + 1 < n_sup)
                run_pass("p", TS, s * TS, Sp, False, s == n_sup - 1, so2=so2, store_y=last_slot,
                         ycol0=(s - 3 * SEGT) * TS if n_sup == NSUP else s * TS,
                         load_x=not pref_done[0], prefetch=("p", (s + 1) * TS) if can_pref else None)
                pref_done[0] = can_pref
                if (s + 1) % SEGT == 0 and s + 1 < n_sup:
                    m = (s + 1) // SEGT - 1
                    for l in range(DEPTH):
                        for nm in ("lc", "lh", "gla", "glab", "glab2", "sc", "ssd", "ssdb", "ssdb2", "fc"):
                            t_ = Sp[l][nm]
                            np_ = 64 if nm in ("gla", "glab", "glab2") else 128
                            op("dve", lambda h, t_=t_, m=m, np_=np_: h.tensor_scalar(out=t_[:], in0=t_[:], scalar1=segmask[0:np_, m:m + 1], scalar2=None, op0=ALU.mult),
                               reads=[Sp[l]["b_" + nm], b_segmask], writes=[Sp[l]["b_" + nm]])
        if do_sample:
            run_pass("s", 64, 0, Sp, True, True)
        fw.finish()
        build.ninstr = fw.ninstr
    return nc


def _fm(v, nchunk):
    return np.ascontiguousarray(np.asarray(v, np.float32).reshape(nchunk, 128).T)


def _consts():
    c = {}
    c["c_ident"] = np.eye(128, dtype=np.float32)
    j = np.arange(64)
    c["c_mask"] = (j[:, None] <= j[None, :]).astype(np.float32)
    c["c_U"] = (j[:, None] > j[None, :]).astype(np.float32)
    dg = np.zeros((8, 8, 64), np.float32)
    for h in range(8):
        dg[h, h, :] = 1.0
    c["c_diag"] = dg.reshape(8, 512)
    mn = np.where(j[:, None] <= j[None, :], 0.0, -30000.0).astype(np.float32)
    c["c_mneg"] = np.ascontiguousarray(np.broadcast_to(mn[:, None, :], (64, 8, 64))).reshape(64, 512)
    sel = np.zeros((64, 128), np.float32)
    sel[63, :] = 1.0
    c["c_sel"] = sel
    rm = np.ones((64, TS), np.float32)
    rm[:, ::64] = 0.0
    c["c_rmask"] = rm
    return c


def _prep_shared(inp):
    f = lambda a: np.ascontiguousarray(np.asarray(a, dtype=np.float32))
    sh = {}
    sh["w_in"] = f(inp["w_in"])
    sh["w_out"] = f(inp["w_out"])
    sh["w_gate"] = f(inp["ffn_w_gate"])
    sh["w_up"] = f(inp["ffn_w_up"])
    sh["w_down"] = f(inp["ffn_w_down"])
    sh["ple_wg"] = f(inp["ple_w_gate"])
    sh["ple_wp"] = f(inp["ple_w_proj"])
    norms = np.zeros((DEPTH, 128, 3, 8), np.float32)
    lru = np.zeros((DEPTH, 128, 4, 8), np.float32)
    wr = np.zeros((DEPTH, 128, 4, 128), np.float32)
    wi = np.zeros((DEPTH, 128, 4, 128), np.float32)
    blr = np.zeros((DEPTH, 64, 4), np.float32)
    gnorm = np.zeros((DEPTH, 128, 4), np.float32)
    scw = np.zeros((DEPTH, 128, 8, 5), np.float32)
    sbc = np.zeros((DEPTH, 64, 3, 8), np.float32)
    snorm = np.zeros((DEPTH, 128, 4), np.float32)
    fcw = np.zeros((DEPTH, 128, 24, 4), np.float32)
    for l in range(DEPTH):
        norms[l, :, 0, :] = _fm(inp["norm_mix"][l], 8)
        norms[l, :, 1, :] = _fm(inp["norm_ffn"][l], 8)
        norms[l, :, 2, :] = _fm(inp["norm_ple"][l], 8)
        for j in range(4):
            lru[l, :, :, j] = _fm(inp["lru_conv_w"][l, j], 4)
        lru[l, :, :, 4] = _fm(inp["lru_conv_b"][l], 4)
        lru[l, :, :, 5] = _fm(inp["lru_b_r"][l], 4)
        lru[l, :, :, 6] = _fm(inp["lru_b_i"][l], 4)
        lru[l, :, :, 7] = _fm(inp["lru_lambda"][l], 4)
        for h in range(8):
            c, o = h // 2, (h % 2) * 64
            wr[l, o:o + 64, c, o:o + 64] = inp["lru_w_r"][l, h]
            wi[l, o:o + 64, c, o:o + 64] = inp["lru_w_i"][l, h]
        blr[l] = np.asarray(inp["gla_b_lr"][l], np.float32).reshape(4, 64).T
        gnorm[l] = _fm(inp["gla_norm"][l], 4)
        for j in range(4):
            scw[l, :, :, j] = _fm(inp["ssd_conv_w"][l, j], 8)
        scw[l, :, :, 4] = _fm(inp["ssd_conv_b"][l], 8)
        sbc[l, :, 0, :] = np.asarray(inp["ssd_dt_bias"][l], np.float32)[None, :]
        sbc[l, :, 1, :] = np.asarray(inp["ssd_a_log"][l], np.float32)[None, :]
        sbc[l, :, 2, :] = np.asarray(inp["ssd_d"][l], np.float32)[None, :]
        snorm[l] = _fm(inp["ssd_norm"][l], 4)
        for j in range(3):
            fcw[l, :, :, j] = _fm(inp["ffn_conv_w"][l, j], 24)
        fcw[l, :, :, 3] = _fm(inp["ffn_conv_b"][l], 24)
    sh.update(norms=norms, nfinal=_fm(inp["norm_final"], 8), lru_small=lru, wr_bd=wr, wi_bd=wi, wlr=f(inp["gla_w_lr"]),
              blr=blr, gnorm=gnorm, ssd_cw=scw, ssd_bc=sbc, snorm=snorm, ffn_cw=fcw)
    sh.update(_consts())
    return sh


def _prep_core(inp, c):
    f = lambda a: np.ascontiguousarray(np.asarray(a, dtype=np.float32))
    m = {}
    b, k = c // 4, c % 4
    SEG = SEQ // 4
    segs = [max(s - (3 - k), 0) for s in range(4)]
    xp = np.asarray(inp["x_prompt"][b])
    pp = np.asarray(inp["p_prompt"][:, b])
    m["xT_p"] = f(np.concatenate([xp[g * SEG:(g + 1) * SEG] for g in segs], axis=0).T)
    m["pT_p"] = f(np.transpose(np.concatenate([pp[:, g * SEG:(g + 1) * SEG] for g in segs], axis=1), (0, 2, 1)))
    sm = np.ones((128, 4), np.float32)
    for mb in range(3):
        if mb < 3 - k:
            sm[:, mb] = 0.0
    m["segmask"] = sm
    m["xT_s"] = f(np.asarray(inp["x_sample"][c]).T)
    m["pT_s"] = f(np.transpose(np.asarray(inp["p_sample"][:, c]), (0, 2, 1)))
    lc = np.asarray(inp["state_lru_conv"][:, c], np.float32)
    m["s_lc"] = f(lc.reshape(DEPTH, 3, 4, 128).transpose(0, 3, 2, 1))
    lh = np.asarray(inp["state_lru_h"][:, c], np.float32)
    m["s_lh"] = f(lh.reshape(DEPTH, 4, 128).transpose(0, 2, 1))
    g = np.asarray(inp["state_gla"][:, c], np.float32)
    m["s_gla"] = f(g.transpose(0, 2, 1, 3))
    sc = np.asarray(inp["state_ssd_conv"][:, c], np.float32)
    m["s_sc"] = f(sc.reshape(DEPTH, 3, 8, 128).transpose(0, 3, 2, 1))
    ss = np.asarray(inp["state_ssd"][:, c], np.float32)
    m["s_ssd"] = f(ss.transpose(0, 3, 1, 2))
    fc = np.asarray(inp["state_ffn_conv"][:, c], np.float32)
    m["s_fc"] = f(fc.reshape(DEPTH, 2, 24, 128).transpose(0, 3, 2, 1))
    return m


def _unpack_states(r, sk):
    lc = r["o_lc_" + sk].transpose(0, 3, 2, 1).reshape(DEPTH, 3, 512)
    lh = r["o_lh_" + sk].transpose(0, 2, 1).reshape(DEPTH, 512)
    g = r["o_gla_" + sk].transpose(0, 2, 1, 3)
    sc = r["o_sc_" + sk].transpose(0, 3, 2, 1).reshape(DEPTH, 3, 1024)
    ss = r["o_ssd_" + sk].transpose(0, 2, 3, 1)
    fc = r["o_fc_" + sk].transpose(0, 3, 2, 1).reshape(DEPTH, 2, 3072)
    return [lc, lh, g, sc, ss, fc]


_NC_CACHE = {}


def kernel(**inputs):
    if "nc" not in _NC_CACHE:
        _NC_CACHE["nc"] = build()
    nc = _NC_CACHE["nc"]
    sh = _prep_shared(inputs)
    in_maps = []
    for c in range(8):
        m = dict(sh)
        m.update(_prep_core(inputs, c))
        in_maps.append(m)
    res = run_bass_kernel_spmd(nc, in_maps, core_ids=list(range(8)))
    R = res.results
    y_prompt = np.stack([np.concatenate([R[4 * b + k]["yT_p"].T for k in range(4)], axis=0) for b in range(2)]).astype(np.float32)
    y_sample = np.stack([np.ascontiguousarray(R[c]["yT_s"].T) for c in range(8)]).astype(np.float32)
    pst = [_unpack_states(R[4 * b + 3], "p") for b in range(2)]
    sst = [_unpack_states(R[c], "s") for c in range(8)]
    outs = [y_prompt, y_sample]
    for i in range(6):
        outs.append(np.ascontiguousarray(np.stack([pst[b][i] for b in range(2)], axis=1)).astype(np.float32))
    for i in range(6):
        outs.append(np.ascontiguousarray(np.stack([sst[c][i] for c in range(8)], axis=1)).astype(np.float32))
    return tuple(outs)
```
